# Optimizing a Trainium2 kernel written in Bass

```python
import math
import jax, jax.numpy as jnp
from jax import lax
import numpy as np

D_MODEL = 1024
BATCH = 16
SEQ = 2048
DEPTH = 4

CTX_LEN = 256
GRID_W = 64

N_HEADS = 8
HEAD_DIM = 64
V_DIM = 2 * HEAD_DIM
QK_W = N_HEADS * 2 * HEAD_DIM
ATTN_W = N_HEADS * V_DIM
CONV_W = D_MODEL
CONV_K = 3
N_FGROUPS = 8
FGROUP_DIM = 128
FOURIER_W = N_FGROUPS * FGROUP_DIM
N_BRANCH = 3
ROPE_BASE = 10000.0
AXIS_DIM = HEAD_DIM // 2
Q_BLOCK = 128
NORM_EPS = 1e-6
SUBLN_EPS = 1e-5

SPLIT_SIZES = (QK_W, QK_W, ATTN_W, ATTN_W,
               CONV_W, CONV_W, CONV_W, CONV_W,
               FOURIER_W, FOURIER_W,
               N_BRANCH * D_MODEL)
PROJ_W = 2 * QK_W + 2 * ATTN_W + 4 * CONV_W + 2 * FOURIER_W + N_BRANCH * D_MODEL

kernel_name = "hybrid_diffattn_shortconv_fourier_dit"


def rmsnorm(x, w, eps=NORM_EPS):
    xf = x.astype(jnp.float32)
    y = xf * lax.rsqrt(jnp.mean(xf * xf, axis=-1, keepdims=True) + eps)
    return y.astype(x.dtype) * w


def adaln(cond, w_mod, b_mod):
    mod = jax.nn.silu(cond) @ w_mod + b_mod
    shift, scale, gate = jnp.split(mod, 3, axis=-1)
    return (jnp.expand_dims(shift, -2), jnp.expand_dims(scale, -2), jnp.expand_dims(gate, -2))


def split_proj(p):
    idx = [int(i) for i in np.cumsum(SPLIT_SIZES)[:-1]]
    return jnp.split(p, idx, axis=-1)


def axial_angles(n_tokens):
    rows = n_tokens // GRID_W
    row = jnp.repeat(jnp.arange(rows), GRID_W).astype(jnp.float32)
    col = jnp.tile(jnp.arange(GRID_W), rows).astype(jnp.float32)
    inv_freq = ROPE_BASE ** (-jnp.arange(0, AXIS_DIM, 2, dtype=jnp.float32) / AXIS_DIM)
    return row[:, None] * inv_freq, col[:, None] * inv_freq


def rotate(x, ang):
    cos = jnp.cos(ang).astype(x.dtype)
    sin = jnp.sin(ang).astype(x.dtype)
    x1, x2 = jnp.split(x, 2, axis=-1)
    return jnp.concatenate([x1 * cos - x2 * sin, x2 * cos + x1 * sin], axis=-1)


def axial_rope(x, ang_r, ang_c):
    xr, xc = jnp.split(x, 2, axis=-1)
    return jnp.concatenate([rotate(xr, ang_r), rotate(xc, ang_c)], axis=-1)


def qk_heads(t):
    b, n, _ = t.shape
    return t.reshape(b, n, N_HEADS, 2, HEAD_DIM).transpose(0, 2, 3, 1, 4)


def v_heads(t):
    b, n, _ = t.shape
    return t.reshape(b, n, N_HEADS, V_DIM).transpose(0, 2, 1, 3)


def diff_weights(q, k, lam):
    s = jnp.einsum('bhmqd,bhmkd->bhmqk', q, k).astype(jnp.float32) / math.sqrt(HEAD_DIM)
    p = jax.nn.softmax(s, axis=-1)
    return p[:, :, 0] - lam * p[:, :, 1]


def latent_diff_attn(q, k, v, lam):
    b, h, _, s, d = q.shape
    nblk = s // Q_BLOCK
    qb = jnp.moveaxis(q.reshape(b, h, 2, nblk, Q_BLOCK, d), 3, 0)

    def one_block(qblk):
        a = diff_weights(qblk, k, lam).astype(v.dtype)
        return jnp.einsum('bhqk,bhkv->bhqv', a, v)

    o = lax.map(one_block, qb)
    return jnp.moveaxis(o, 0, 2).reshape(b, h, s, V_DIM)


def context_diff_attn(q, k, v, lam):
    a = diff_weights(q, k, lam).astype(v.dtype)
    return jnp.einsum('bhqk,bhkv->bhqv', a, v)


def subln(o, w, lam_init):
    b, h, n, vd = o.shape
    y = rmsnorm(o, w, SUBLN_EPS) * (1.0 - lam_init)
    return y.transpose(0, 2, 1, 3).reshape(b, n, h * vd)


def short_conv(xin, b_g, c_g, conv_w):
    u = c_g * xin
    up = jnp.pad(u, ((0, 0), (1, 1), (0, 0)))
    y = conv_w[0] * up[:, :-2] + conv_w[1] * up[:, 1:-1] + conv_w[2] * up[:, 2:]
    return b_g * y


def fourier_mix(u):
    b, n, _ = u.shape
    g = u.astype(jnp.float32).reshape(b, n, N_FGROUPS, FGROUP_DIM)
    y = jnp.fft.fft2(g, axes=(1, 3), norm="ortho").real
    return y.reshape(b, n, FOURIER_W).astype(u.dtype)


def other_branches_and_merge(y_a, z_a, xin, b_g, c_g, z_c, u_f, z_f, gate_logits,
                             conv_w, w_a, w_c, w_f, w_o):
    y_c = short_conv(xin, b_g, c_g, conv_w)
    y_f = fourier_mix(u_f)
    g = jax.nn.sigmoid(gate_logits.astype(jnp.float32)).astype(y_a.dtype)
    g_a, g_c, g_f = jnp.split(g, 3, axis=-1)
    merged = (g_a * ((y_a * jax.nn.silu(z_a)) @ w_a)
              + g_c * ((y_c * jax.nn.silu(z_c)) @ w_c)
              + g_f * ((y_f * jax.nn.silu(z_f)) @ w_f))
    return merged @ w_o


def setup_inputs(seed: int = 0) -> dict:
    key = jax.random.key(seed)
    ks = jax.random.split(key, 16)
    f32 = jnp.float32
    nrm = lambda k, shp: jax.random.normal(k, shp, f32)
    return {
        "x": nrm(ks[0], (BATCH, SEQ, D_MODEL)),
        "c": nrm(ks[1], (BATCH, D_MODEL)),
        "ctx": nrm(ks[2], (BATCH, CTX_LEN, D_MODEL)),
        "c_ctx": nrm(ks[3], (D_MODEL,)),
        "norm_w": 1.0 + 0.02 * nrm(ks[4], (DEPTH, D_MODEL)),
        "w_mod": nrm(ks[5], (DEPTH, D_MODEL, 3 * D_MODEL)) * (0.5 * D_MODEL ** -0.5),
        "b_mod": 0.01 * nrm(ks[6], (DEPTH, 3 * D_MODEL)),
        "w_in": nrm(ks[7], (DEPTH, D_MODEL, PROJ_W)) * D_MODEL ** -0.5,
        "lambda_qk": 0.1 * nrm(ks[8], (DEPTH, 4, HEAD_DIM)),
        "subln_w": 1.0 + 0.02 * nrm(ks[9], (DEPTH, V_DIM)),
        "conv_w": nrm(ks[10], (DEPTH, CONV_K, CONV_W)) * CONV_K ** -0.5,
        "w_attn_o": nrm(ks[11], (DEPTH, ATTN_W, D_MODEL)) * ATTN_W ** -0.5,
        "w_conv_o": nrm(ks[12], (DEPTH, CONV_W, D_MODEL)) * CONV_W ** -0.5,
        "w_four_o": nrm(ks[13], (DEPTH, FOURIER_W, D_MODEL)) * FOURIER_W ** -0.5,
        "w_out": nrm(ks[14], (DEPTH, D_MODEL, D_MODEL)) * D_MODEL ** -0.5,
        "final_norm_w": 1.0 + 0.02 * nrm(ks[15], (D_MODEL,)),
    }


def reference(x, c, ctx, c_ctx, norm_w, w_mod, b_mod, w_in, lambda_qk, subln_w, conv_w,
              w_attn_o, w_conv_o, w_four_o, w_out, final_norm_w):
    ang_r, ang_c = axial_angles(x.shape[1])
    for l in range(DEPTH):
        last = l == DEPTH - 1
        lam_init = 0.8 - 0.6 * math.exp(-0.3 * l)
        lq = lambda_qk[l].astype(jnp.float32)
        lam = jnp.exp(jnp.sum(lq[0] * lq[1])) - jnp.exp(jnp.sum(lq[2] * lq[3])) + lam_init

        sh, sc, gt = adaln(c, w_mod[l], b_mod[l])
        shc, scc, gtc = adaln(c_ctx, w_mod[l], b_mod[l])
        h = rmsnorm(x, norm_w[l]) * (1.0 + sc) + sh
        hc = rmsnorm(ctx, norm_w[l]) * (1.0 + scc) + shc

        (q, k, v, z_a, xin, b_g, c_g, z_c, u_f, z_f, gl) = split_proj(h @ w_in[l])
        (qc, kc, vc, z_ac, xinc, b_gc, c_gc, z_cc, u_fc, z_fc, glc) = split_proj(hc @ w_in[l])

        q_h = axial_rope(qk_heads(q), ang_r, ang_c)
        k_h = axial_rope(qk_heads(k), ang_r, ang_c)
        qc_h, kc_h, vc_h = qk_heads(qc), qk_heads(kc), v_heads(vc)
        k_all = jnp.concatenate([k_h, kc_h], axis=3)
        v_all = jnp.concatenate([v_heads(v), vc_h], axis=2)
        y_a = subln(latent_diff_attn(q_h, k_all, v_all, lam), subln_w[l], lam_init)

        out = other_branches_and_merge(y_a, z_a, xin, b_g, c_g, z_c, u_f, z_f, gl,
                                       conv_w[l], w_attn_o[l], w_conv_o[l], w_four_o[l], w_out[l])

        if not last:
            y_ac = subln(context_diff_attn(qc_h, kc_h, vc_h, lam), subln_w[l], lam_init)
            out_c = other_branches_and_merge(y_ac, z_ac, xinc, b_gc, c_gc, z_cc, u_fc, z_fc, glc,
                                             conv_w[l], w_attn_o[l], w_conv_o[l], w_four_o[l], w_out[l])
            ctx = ctx + gtc * out_c

        x = x + gt * out
    return rmsnorm(x, final_norm_w)
```

```python
import math
from contextlib import ExitStack

import numpy as np
import ml_dtypes
import concourse.bass as bass
import concourse.mybir as mybir
from concourse.bass_utils import run_bass_kernel_spmd

F32 = mybir.dt.float32
BF16 = mybir.dt.bfloat16
AF = mybir.ActivationFunctionType
ALU = mybir.AluOpType
AX = mybir.AxisListType

D = 1024
SEQ = 2048
CTX = 256
NT = SEQ + CTX
DEPTH = 4
PROJ_W = 13312
N_CORES = 8
TB = [(0, 512), (512, 1024), (1024, 1536), (1536, 2048), (2048, 2304)]
NORM_EPS = 1e-6
SUBLN_EPS = 1e-5


class Reg:
    __slots__ = ("w", "r", "name")

    def __init__(self, name=""):
        self.w = {}
        self.r = {}
        self.name = name


class Sched:
    ENG = ("pe", "act", "dve", "pool", "sp")

    def __init__(self):
        self.ops = {k: [] for k in self.ENG}
        self.pending = {k: {} for k in self.ENG}
        self.dma_cnt = {}
        self.out_tokens = {}

    @staticmethod
    def _merge(dst, src):
        for k, v in src.items():
            if dst.get(k, -1) < v:
                dst[k] = v

    def _record(self, eng, fn, reads, writes, dma_sem=None):
        deps = {}
        for r in reads:
            self._merge(deps, r.w)
        for w in writes:
            self._merge(deps, w.w)
            self._merge(deps, w.r)
        if dma_sem is not None:
            for w in writes:
                k = ("d", dma_sem)
                if k in w.w and deps.get(k) == w.w[k]:
                    deps.pop(k)
        self._merge(deps, self.pending[eng])
        self.pending[eng] = {}
        if eng == "pe":
            deps.pop(("e", "pe"), None)
        idx = len(self.ops[eng])
        for (kind, name), v in deps.items():
            if kind == "e":
                self.ops[name][v]["signal"] = True
        o = dict(fn=fn, deps=deps, signal=False, dma=dma_sem)
        self.ops[eng].append(o)
        if dma_sem is not None:
            self.dma_cnt[dma_sem] = self.dma_cnt.get(dma_sem, 0) + 16
            key, val = ("d", dma_sem), self.dma_cnt[dma_sem]
            self.out_tokens[key] = val
        else:
            key, val = ("e", eng), idx
        for r in reads:
            if r.r.get(key, -1) < val:
                r.r[key] = val
        for w in writes:
            w.w = {key: val}
            w.r = {}
        return o

    def op(self, eng, fn, reads=(), writes=()):
        return self._record(eng, fn, reads, writes)

    def dma(self, fn, sem, reads=(), writes=()):
        return self._record("sp", fn, reads, writes, dma_sem=sem)

    def barrier(self):
        deps = {}
        for e in ("pe", "act", "dve", "pool"):
            if self.ops[e]:
                deps[("e", e)] = len(self.ops[e]) - 1
        self._merge(deps, self.out_tokens)
        for e in self.ENG:
            d = dict(deps)
            d.pop(("e", e), None)
            for (kind, name), v in d.items():
                if kind == "e":
                    self.ops[name][v]["signal"] = True
            self._merge(self.pending[e], d)

    def emit(self, nc, block, esem, dsem, final_waits):
        handles = {"pe": block.tensor, "act": block.scalar, "dve": block.vector,
                   "pool": block.gpsimd, "sp": block.sync}
        cnt = {}
        for e in self.ENG:
            c = 0
            arr = []
            for o in self.ops[e]:
                if o["signal"]:
                    c += 1
                arr.append(c)
            cnt[e] = arr

        def make(e):
            ops = self.ops[e]

            def body(eng):
                waited = {}
                for o in ops:
                    for (kind, name), v in o["deps"].items():
                        if kind == "e":
                            sem, val = esem[name], cnt[name][v]
                        else:
                            sem, val = dsem[name], v
                        key = (kind, name)
                        if waited.get(key, 0) >= val:
                            continue
                        eng.wait_ge(sem, val)
                        waited[key] = val
                    ins = o["fn"](eng)
                    if o["dma"] is not None:
                        ins.then_inc(dsem[o["dma"]], 16)
                    elif o["signal"]:
                        ins.then_inc(esem[e], 1)
                if e == "sp":
                    for name in final_waits:
                        if name in self.dma_cnt:
                            eng.wait_ge(dsem[name], self.dma_cnt[name])
            return body

        for e in self.ENG:
            handles[e](make(e))


def I(method, *args, **kwargs):
    def fn(e):
        return getattr(e, method)(*args, **kwargs)
    return fn


def build_program(NB=2, NL=DEPTH):
    nc = bass.Bass("TRN2", target_bir_lowering=False)
    S = Sched()

    def din(name, shape, dt=F32):
        return nc.dram_tensor(name, list(shape), dt, kind="ExternalInput").ap()

    x_d = din("x", [NB, SEQ, D])
    ctx_d = din("ctx", [NB, CTX, D])
    c_d = din("c", [NB, D])
    cctx_d = din("c_ctx", [D])
    normw_d = din("norm_w", [DEPTH, D])
    wmod_d = din("w_mod", [DEPTH, D, 3 * D])
    bmod_d = din("b_mod", [DEPTH, 3 * D])
    win_d = din("w_in", [DEPTH, D, PROJ_W])
    lam_d = din("lambda_qk", [DEPTH, 256])
    subln_d = din("subln_w", [DEPTH, 128])
    convw_d = din("conv_w", [DEPTH, 3, D])
    wa_d = din("w_attn_o", [DEPTH, D, D])
    wc_d = din("w_conv_o", [DEPTH, D, D])
    wf_d = din("w_four_o", [DEPTH, D, D])
    wo_d = din("w_out", [DEPTH, D, D])
    fnw_d = din("final_norm_w", [D])
    dftc_d = din("t_dftc", [SEQ, SEQ], BF16)
    dfts_d = din("t_dfts", [SEQ, SEQ], BF16)
    dftcc_d = din("t_dftc_c", [CTX, CTX], BF16)
    dftsc_d = din("t_dfts_c", [CTX, CTX], BF16)
    csc_d = din("t_csc", [128, 256], BF16)
    ropec_d = din("t_ropec", [128, SEQ])
    ropes_d = din("t_ropes", [128, SEQ])
    identb_d = din("t_identb", [128, 128], BF16)
    identf_d = din("t_identf", [128, 128])
    perm_d = din("t_perm", [128, 128])
    permb_d = din("t_permb", [128, 128], BF16)
    xs_d = nc.dram_tensor("xs", [NT, D], F32, kind="Internal").ap()
    out_d = nc.dram_tensor("out", [NB, SEQ, D], F32, kind="ExternalOutput").ap()

    es = ExitStack()

    def sb(name, shape, dt=F32):
        return es.enter_context(nc.sbuf_tensor(name, list(shape), dt))

    def psum(name, shape, dt=F32):
        return es.enter_context(nc.psum_tensor(name, list(shape), dt))

    hT = sb("hT", [128, 8, NT], BF16)
    Yb = sb("Yb", [128, 8, NT], BF16)
    Mb = sb("Mb", [128, 8, NT], BF16)
    r_hT = [Reg("hT%d" % i) for i in range(5)]
    r_Y = [[Reg() for _ in range(5)] for _ in range(8)]
    r_M = [[Reg() for _ in range(5)] for _ in range(8)]

    identb = sb("identb", [128, 128], BF16)
    identf = sb("identf", [128, 128])
    perm = sb("perm", [128, 128])
    permb = sb("permb", [128, 128], BF16)
    ropec = sb("ropec", [128, SEQ])
    ropes = sb("ropes", [128, SEQ])
    csc = sb("csc", [128, 256], BF16)
    dftcc = sb("dftcc", [128, 2, CTX], BF16)
    dftsc = sb("dftsc", [128, 2, CTX], BF16)
    ones_f = sb("ones_f", [128, 128])
    fnw_b = sb("fnw_b", [128, D])
    r_const = Reg("const")

    cT = sb("cT", [128, 8, 2])
    r_cT = Reg("cT")
    normwT = sb("normwT", [128, 8])
    bmodT = sb("bmodT", [128, 24])
    convwT = sb("convwT", [128, 3, 8])
    sublnb = sb("sublnb", [128, 128])
    lamb = sb("lamb", [128, 256])
    r_lay = Reg("lay")
    gam = sb("gam", [128, 8, 2])
    bet = sb("bet", [128, 8, 2])
    r_gb = Reg("gb")
    nlam = sb("nlam", [128, 1])
    lamtmp = sb("lamtmp", [128, 2, 64])
    lam2 = sb("lam2", [128, 2])
    r_lam = Reg("lam")

    NST, NUS = 3, 8
    wst = sb("wst", [128, NST, 8, 128])
    r_wst = [Reg("wst%d" % i) for i in range(NST)]
    wub = sb("wub", [128, NUS, 8, 128], BF16)
    r_wub = [Reg("wub%d" % i) for i in range(NUS)]

    ARENA = 43 * 1024
    arena = sb("arena", [128, ARENA // 4])

    class Carver:
        def __init__(self):
            self.off = 0

        def take(self, shape, dt=F32):
            n = 1
            for s_ in shape:
                n *= s_
            esz = 4 if dt == F32 else 2
            nbytes = (n * esz + 31) // 32 * 32
            assert self.off + nbytes <= ARENA, ("arena overflow", self.off, nbytes)
            w0 = self.off // 4
            ap = arena[:, w0:w0 + nbytes // 4]
            if dt != F32:
                ap = ap.bitcast(dt)
            ap = ap[:, 0:n]
            if len(shape) == 2:
                pat = "p (a b) -> p a b"
                ap = ap.rearrange(pat, a=shape[0], b=shape[1])
            elif len(shape) == 3:
                ap = ap.rearrange("p (a b c) -> p a b c", a=shape[0], b=shape[1], c=shape[2])
            self.off += nbytes
            return ap

    banks = [psum("pb%d" % i, [128, 512]) for i in range(7)]
    r_bank = [Reg("bank%d" % i) for i in range(7)]
    pst = psum("pst", [128, 1024], BF16)
    r_pst = [Reg("pst%d" % i) for i in range(4)]
    rot = {}

    def next_bank(pool=(0, 1, 2, 3, 4)):
        k = rot.get(pool, 0)
        rot[pool] = (k + 1) % len(pool)
        i = pool[k]
        return banks[i], r_bank[i]

    flip = {"i": 0}

    def evac_eng():
        flip["i"] ^= 1
        return "act" if flip["i"] else "dve"

    def copy_op(eng, out, in_, reads, writes):
        if eng == "act":
            S.op("act", I("activation", out, in_, AF.Copy), reads, writes)
        elif eng == "dve":
            S.op("dve", I("tensor_copy", out, in_), reads, writes)
        else:
            S.op("pool", I("tensor_copy", out, in_), reads, writes)

    def mm(out, lhsT, rhs, start, stop, reads, writes, skip=False):
        S.op("pe", I("matmul", out, lhsT, rhs, start=start, stop=stop, skip_group_check=skip),
             reads, writes)

    class Units:
        def __init__(self):
            self.n = 0
            self.released = {}
            self.pref = {}

        def prefetch(self, key, src_ap):
            if key not in self.pref:
                self.pref[key] = self.request(src_ap)

        def request(self, src_ap, key=None):
            if key is not None and key in self.pref:
                return self.pref.pop(key)
            u = self.n
            self.n += 1
            if u >= NUS:
                assert self.released.get(u - NUS, False), "unit slot reuse before release"
            st = u % NST
            sl = u % NUS
            src = src_ap.rearrange("(kc p) c -> p kc c", p=128)
            S.dma(I("dma_start", out=wst[:, st], in_=src), "wst%d" % st,
                  reads=(), writes=(r_wst[st],))
            S.op("pool", I("tensor_copy", wub[:, sl], wst[:, st]),
                 reads=(r_wst[st],), writes=(r_wub[sl],))
            return u

        def ap(self, u):
            return wub[:, u % NUS], r_wub[u % NUS]

        def release(self, u):
            self.released[u] = True

    U = Units()

    def win_cols(l, c0):
        return win_d[l, :, c0:c0 + 128]

    def req_win(l, c0, pre=False):
        key = ("win", l, c0)
        if pre:
            return U.prefetch(key, win_cols(l, c0))
        return U.request(win_cols(l, c0), key)

    WB = {0: wa_d, 1: wc_d, 2: wf_d}

    def req_wb(l, br, j, pre=False):
        key = ("wb", l, br, j)
        ap = WB[br][l, :, j * 128:(j + 1) * 128]
        if pre:
            return U.prefetch(key, ap)
        return U.request(ap, key)

    def req_wo(l, ob, pre=False):
        key = ("wo", l, ob)
        ap = wo_d[l, :, ob * 128:(ob + 1) * 128]
        if pre:
            return U.prefetch(key, ap)
        return U.request(ap, key)

    def prefetch_attn(l):
        for c in range(4):
            req_win(l, c * 1024, pre=True)

    def prefetch_merge(l, br):
        req_wb(l, br, 0, pre=True)
        req_win(l, 10240 + br * 1024, pre=True)

    def cdma(out, in_, **kw):
        S.dma(I("dma_start", out=out, in_=in_, **kw), "const", reads=(), writes=(r_const,))

    cdma(identb[:], identb_d)
    cdma(identf[:], identf_d)
    cdma(perm[:], perm_d)
    cdma(permb[:], permb_d)
    cdma(ropec[:], ropec_d)
    cdma(ropes[:], ropes_d)
    cdma(csc[:], csc_d)
    cdma(dftcc[:], dftcc_d.rearrange("(t p) k -> p t k", p=128))
    cdma(dftsc[:], dftsc_d.rearrange("(t p) k -> p t k", p=128))
    cdma(fnw_b[:], fnw_d.partition_broadcast(128))
    S.op("dve", I("memset", ones_f[:], 1.0), (), (r_const,))

    def load_layer_small(l):
        def d(out, in_, **kw):
            S.dma(I("dma_start", out=out, in_=in_, **kw), "lay", reads=(), writes=(r_lay,))
        d(normwT[:], normw_d[l].rearrange("(j p) -> p j", p=128), allow_slow_non_contiguous=True)
        d(bmodT[:], bmod_d[l].rearrange("(j p) -> p j", p=128), allow_slow_non_contiguous=True)
        d(convwT[:], convw_d[l].rearrange("k (c p) -> p k c", p=128), allow_slow_non_contiguous=True)
        d(sublnb[:], subln_d[l].partition_broadcast(128))
        d(lamb[:], lam_d[l].partition_broadcast(128))

    def compute_lam(l):
        lam_init = 0.8 - 0.6 * math.exp(-0.3 * l)
        lv = lamb[:].rearrange("p (a b d) -> p a b d", a=2, b=2, d=64)
        S.op("dve", I("tensor_tensor", lamtmp[:], lv[:, :, 0, :], lv[:, :, 1, :], ALU.mult),
             (r_lay,), (r_lam,))
        S.op("dve", I("tensor_reduce", lam2[:], lamtmp[:], AX.X, ALU.add), (r_lam,), (r_lam,))
        S.op("act", I("activation", lam2[:], lam2[:], AF.Exp), (r_lam,), (r_lam,))
        S.op("dve", I("tensor_tensor", nlam[:], lam2[:, 1:2], lam2[:, 0:1], ALU.subtract),
             (r_lam,), (r_lam,))
        S.op("dve", I("tensor_scalar", nlam[:], nlam[:], -lam_init, None, ALU.add),
             (r_lam,), (r_lam,))
        S.op("dve", I("tensor_scalar", sublnb[:], sublnb[:], 1.0 - lam_init, None, ALU.mult),
             (r_lay,), (r_lay,))

    def load_c(b):
        S.dma(I("dma_start", out=cT[:, :, 0], in_=c_d[b].rearrange("(j p) -> p j", p=128),
                                    allow_slow_non_contiguous=True), "cload", (), (r_cT,))
        S.dma(I("dma_start", out=cT[:, :, 1], in_=cctx_d.rearrange("(j p) -> p j", p=128),
                                    allow_slow_non_contiguous=True), "cload", (), (r_cT,))
        S.op("act", I("activation", cT[:], cT[:], AF.Silu), (r_cT,), (r_cT,))

    def mods_ss(l):
        bk, rb = banks[5], r_bank[5]
        first = True
        for kc in range(8):
            for half in range(2):
                st = (kc * 2 + half) % NST
                stv = wst[:, st].rearrange("p a b -> p (a b)")
                src = wmod_d[l, kc * 128:(kc + 1) * 128, half * 1024:(half + 1) * 1024]
                S.dma(I("dma_start", out=stv, in_=src), "wst%d" % st, (), (r_wst[st],))
                for cc in range(8):
                    col = (half * 8 + cc) * 2
                    o_ = bk[:, col:col + 2]
                    l_ = stv[:, cc * 128:(cc + 1) * 128]
                    r_ = cT[:, kc, :]
                    mm(o_, l_, r_, first, kc == 7, (r_wst[st], r_cT), (rb,), skip=True)
                    first = False
        pv = bk[:, 0:32].rearrange("p (h c s) -> p h c s", h=2, c=8, s=2)
        bm = bmodT[:]
        for s_ in range(2):
            S.op("dve", I("tensor_tensor", bet[:, :, s_], pv[:, 0, :, s_], bm[:, 0:8], ALU.add),
                 (rb, r_lay), (r_gb,))
            S.op("dve", I("scalar_tensor_tensor", gam[:, :, s_], pv[:, 1, :, s_], 1.0, bm[:, 8:16],
                                                                 ALU.add, ALU.add), (rb, r_lay), (r_gb,))
            S.op("dve", I("tensor_tensor", gam[:, :, s_], gam[:, :, s_], normwT[:], ALU.mult),
                 (r_gb, r_lay), (r_gb,))

    def norm_tile(xsrc, r_x, t, scr):
        s_ = 0 if t < 16 else 1
        blk = t // 4 if t < 16 else 4
        junk, ss, xn, r_ss, r_xn = scr
        S.op("act", I("activation", junk, xsrc, AF.Square, accum_out=ss), (r_x,), (r_ss,))
        S.op("dve", I("tensor_scalar", ss, ss, 1.0 / D, NORM_EPS, ALU.mult, ALU.add), (r_ss,), (r_ss,))
        S.op("act", I("activation", ss, ss, AF.Sqrt), (r_ss,), (r_ss,))
        S.op("dve", I("reciprocal", ss, ss), (r_ss,), (r_ss,))
        S.op("act", I("activation", xn, xsrc, AF.Copy, scale=ss), (r_x, r_ss), (r_xn,))
        for hf in range(2):
            bk, rb = next_bank()
            for jj in range(4):
                j = hf * 4 + jj
                S.op("pe", I("transpose", bk[:, jj * 128:(jj + 1) * 128],
                                                                    xn[:, j * 128:(j + 1) * 128], identf[:]),
                     (r_xn, r_const), (rb,))
            for jj in range(4):
                j = hf * 4 + jj
                S.op("act", I("activation",
                    hT[:, j, t * 128:(t + 1) * 128], bk[:, jj * 128:(jj + 1) * 128], AF.Identity,
                    bias=bet[:, j, s_:s_ + 1], scale=gam[:, j, s_:s_ + 1]),
                    (rb, r_gb), (r_hT[blk],))

    def proj_fm(u, T, bk, rb):
        wap, rw = U.ap(u)
        t0, t1 = TB[T]
        n = t1 - t0
        for kc in range(8):
            mm(bk[:, 0:n], wap[:, kc, :], hT[:, kc, t0:t1], kc == 0, kc == 7, (rw, r_hT[T]), (rb,))
        return n

    def phase_attn(l, last):
        S.barrier()
        cv = Carver()
        sets = []
        sets.append(dict(qT=cv.take([NT], BF16), kT0=cv.take([NT], BF16), kT1=cv.take([NT], BF16),
                         szT=cv.take([NT], BF16), vaug=cv.take([18, 129], BF16)))
        sets.append(dict(qT=Mb[:, 0, :], kT0=Mb[:, 1, :], kT1=Mb[:, 2, :], szT=Mb[:, 3, :],
                         vaug=Mb[:, 4:6, :].rearrange("p a b -> p (a b)")[:, 0:18 * 129].rearrange(
                             "p (a b) -> p a b", a=18, b=129)))
        for st in sets:
            st["r_q"], st["r_k"], st["r_sz"], st["r_v"] = Reg(), Reg(), Reg(), Reg()
        qhi = [Mb[:, 6, 0:512], Mb[:, 6, 512:1024]]
        qlo = [Mb[:, 6, 1024:1536], Mb[:, 6, 1536:2048]]
        qf = [cv.take([512]) for _ in range(2)]
        t1b = [cv.take([512]) for _ in range(2)]
        Et = [cv.take([512], BF16) for _ in range(3)]
        NSLOT = 6
        osb = [cv.take([128]) for _ in range(NSLOT)]
        ybf = [cv.take([128], BF16) for _ in range(NSLOT)]
        sm = [cv.take([8]) for _ in range(NSLOT)]
        junk = cv.take([128])
        r_qh = [Reg(), Reg()]
        r_qf = [Reg(), Reg()]
        r_t1 = [Reg(), Reg()]
        r_E = [Reg(), Reg(), Reg()]
        r_o = [Reg() for _ in range(NSLOT)]
        r_yb = [Reg() for _ in range(NSLOT)]
        r_sm = [Reg() for _ in range(NSLOT)]
        r_junk = Reg()
        for st in sets:
            S.op("pool", I("memset", st["vaug"][:, :, 128:129], 1.0), (), (st["r_v"],))
            S.op("pool", I("memset", st["kT0"][64:128, :], 0.0), (), (st["r_k"],))
            S.op("pool", I("memset", st["kT1"][0:64, :], 0.0), (), (st["r_k"],))
        ALLB = (0, 1, 2, 3, 4, 5, 6)
        SCB = (0, 1, 2)
        ACC = (3, 4)
        PB = (5, 6)
        ctr = {"qf": 0, "E": 0, "o": 0}

        def head_units(hd):
            return [req_win(l, c * 1024 + hd * 128) for c in range(4)]

        def proj_gen(hd, st, units, pool):
            uq, uk, uv, uz = units
            for T in range(5):
                t0, t1 = TB[T]
                n = t1 - t0
                for (u, isq) in ((uq, True), (uk, False)):
                    rd = st["r_q"] if isq else st["r_k"]
                    bk, rb = next_bank(pool)
                    proj_fm(u, T, bk, rb)
                    if T < 4:
                        i = ctr["qf"] % 2
                        ctr["qf"] += 1
                        S.op("act", I("activation", qhi[i][:, 0:n], bk[:, 0:n], AF.Copy), (rb,), (r_qh[i],))
                        S.op("dve", I("tensor_tensor", qlo[i][:, 0:n], bk[:, 0:n], qhi[i][:, 0:n], ALU.subtract),
                             (rb, r_qh[i]), (r_qh[i],))
                        S.op("dve", I("tensor_tensor", t1b[i][:, 0:n], bk[:, 0:n], ropec[:, t0:t1], ALU.mult),
                             (rb, r_const), (r_t1[i],))
                        yield
                        bk2, rb2 = next_bank(pool)
                        mm(bk2[:, 0:n], permb[:], qhi[i][:, 0:n], True, False, (r_qh[i], r_const), (rb2,))
                        mm(bk2[:, 0:n], permb[:], qlo[i][:, 0:n], False, True, (r_qh[i], r_const), (rb2,))
                        S.op("dve", I("tensor_tensor", qf[i][:, 0:n], bk2[:, 0:n], ropes[:, t0:t1], ALU.mult),
                             (rb2, r_const), (r_qf[i],))
                        if isq:
                            S.op("pool", I("tensor_tensor", st["qT"][:, t0:t1], t1b[i][:, 0:n], qf[i][:, 0:n],
                                           ALU.add), (r_t1[i], r_qf[i]), (rd,))
                        else:
                            S.op("pool", I("tensor_tensor", st["kT0"][0:64, t0:t1], t1b[i][0:64, 0:n],
                                           qf[i][0:64, 0:n], ALU.add), (r_t1[i], r_qf[i]), (rd,))
                            S.op("pool", I("tensor_tensor", st["kT1"][64:128, t0:t1], t1b[i][64:128, 0:n],
                                           qf[i][64:128, 0:n], ALU.add), (r_t1[i], r_qf[i]), (rd,))
                    else:
                        if isq:
                            copy_op("dve", st["qT"][:, t0:t1], bk[:, 0:n], (rb,), (rd,))
                        else:
                            copy_op("dve", st["kT0"][0:64, t0:t1], bk[0:64, 0:n], (rb,), (rd,))
                            copy_op("dve", st["kT1"][64:128, t0:t1], bk[64:128, 0:n], (rb,), (rd,))
                    yield
                bk, rb = next_bank(pool)
                proj_fm(uz, T, bk, rb)
                S.op("act", I("activation", st["szT"][:, t0:t1], bk[:, 0:n], AF.Silu), (rb,), (st["r_sz"],))
                yield
                wv, rwv = U.ap(uv)
                ntile = n // 128
                bk, rb = next_bank(pool)
                for tt in range(ntile):
                    tok0 = t0 + tt * 128
                    for kc in range(8):
                        mm(bk[:, tt * 128:(tt + 1) * 128], hT[:, kc, tok0:tok0 + 128], wv[:, kc, :],
                           kc == 0 and tt == 0, kc == 7, (rwv, r_hT[T]), (rb,), skip=True)
                    if tt == 1 and ntile == 4:
                        yield
                tile0 = t0 // 128
                S.op("dve", I("tensor_copy", st["vaug"][:, tile0:tile0 + ntile, 0:128],
                              bk[:, 0:ntile * 128].rearrange("p (a b) -> p a b", a=ntile, b=128)),
                     (rb,), (st["r_v"],))
                yield
            for u in units:
                U.release(u)

        def drain(gen):
            for _ in gen:
                pass

        drain(proj_gen(0, sets[0], head_units(0), ALLB))

        for hd in range(8):
            st = sets[hd % 2]
            qT, kT0, kT1, szT, vaug = st["qT"], st["kT0"], st["kT1"], st["szT"], st["vaug"]
            r_q, r_k, r_sz, r_v = st["r_q"], st["r_k"], st["r_sz"], st["r_v"]
            gen = proj_gen(hd + 1, sets[(hd + 1) % 2], head_units(hd + 1), PB) if hd < 7 else None

            qblocks = [(q0, list(range(18))) for q0 in range(0, SEQ, 256)]
            if not last:
                qblocks.append((SEQ, [16, 17]))
            stageB = []

            def flush():
                for (oi, qa) in stageB:
                    sm_ = sm[oi]
                    S.op("act", I("activation", sm_[:, 3:4], sm_[:, 2:3], AF.Sqrt), (r_sm[oi],), (r_sm[oi],))
                for (oi, qa) in stageB:
                    sm_ = sm[oi]
                    S.op("dve", I("reciprocal", sm_[:, 3:4], sm_[:, 3:4]), (r_sm[oi],), (r_sm[oi],))
                    S.op("dve", I("scalar_tensor_tensor", ybf[oi][:], osb[oi][:], sm_[:, 3:4], sublnb[:],
                                  ALU.mult, ALU.mult), (r_o[oi], r_sm[oi], r_lay), (r_yb[oi],))
                    pcol = (oi % 4) * 128
                    S.op("pe", I("transpose", pst[:, pcol:pcol + 128], ybf[oi][:], identb[:]),
                         (r_yb[oi], r_const), (r_pst[oi % 4],))
                    blk = qa // 512 if qa < SEQ else 4
                    S.op("dve", I("tensor_tensor", Yb[:, hd, qa:qa + 128], pst[:, pcol:pcol + 128],
                                  szT[:, qa:qa + 128], ALU.mult), (r_pst[oi % 4], r_sz), (r_Y[hd][blk],))
                del stageB[:]

            def scores(q0, kt):
                bk, rb = next_bank(SCB)
                for m in range(2):
                    kTm = kT0 if m == 0 else kT1
                    mm(bk[:, m * 256:(m + 1) * 256], kTm[:, kt * 128:(kt + 1) * 128],
                       qT[:, q0:q0 + 256], True, True, (r_q, r_k), (rb,), skip=True)
                return bk, rb

            steps = [(bi, ki) for bi, (q0_, kts_) in enumerate(qblocks) for ki in range(len(kts_))]
            LOOK = 2
            EVERY = 4
            sc = {}

            def issue(idx):
                bi_, ki_ = steps[idx]
                sc[idx] = scores(qblocks[bi_][0], qblocks[bi_][1][ki_])

            def advance():
                if gen is not None:
                    next(gen, None)

            for idx in range(min(LOOK, len(steps))):
                issue(idx)
            accs = [(banks[ACC[0]], r_bank[ACC[0]]), (banks[ACC[1]], r_bank[ACC[1]])]
            for idx, (bi, ki) in enumerate(steps):
                q0, kts = qblocks[bi]
                kt = kts[ki]
                if idx + LOOK < len(steps):
                    issue(idx + LOOK)
                if ki == 0 or idx % EVERY == 0:
                    advance()
                bk, rb = sc.pop(idx)
                ei = ctr["E"] % 3
                ctr["E"] += 1
                S.op("act", I("activation", Et[ei][:], bk[:], AF.Exp, scale=0.125), (rb,), (r_E[ei],))
                for qs in range(2):
                    ab, rab = accs[qs]
                    for m in range(2):
                        mm(ab[:, m * 256:m * 256 + 129],
                           Et[ei][:, m * 256 + qs * 128:m * 256 + (qs + 1) * 128], vaug[:, kt, :],
                           ki == 0 and m == 0, ki == len(kts) - 1, (r_E[ei], r_v), (rab,), skip=True)
                if ki != len(kts) - 1:
                    continue
                flush()
                ois = []
                for qs in range(2):
                    ab, rab = accs[qs]
                    oi = ctr["o"] % NSLOT
                    ctr["o"] += 1
                    ois.append(oi)
                    sm_ = sm[oi]
                    den = ab[:, 0:512].rearrange("p (m c) -> p m c", m=2, c=256)[:, :, 128]
                    S.op("dve", I("reciprocal", sm_[:, 0:2], den), (rab,), (r_sm[oi],))
                    S.op("dve", I("tensor_tensor", sm_[:, 1:2], sm_[:, 1:2], nlam[:], ALU.mult),
                         (r_sm[oi], r_lam), (r_sm[oi],))
                    S.op("dve", I("tensor_scalar", osb[oi][:], ab[:, 0:128], sm_[:, 0:1], None, ALU.mult),
                         (rab, r_sm[oi]), (r_o[oi],))
                    S.op("dve", I("scalar_tensor_tensor", osb[oi][:], ab[:, 256:384], sm_[:, 1:2], osb[oi][:],
                                  ALU.mult, ALU.add), (rab, r_sm[oi], r_o[oi]), (r_o[oi],))
                for qs in range(2):
                    oi = ois[qs]
                    sm_ = sm[oi]
                    S.op("dve", I("tensor_tensor", junk[:], osb[oi][:], osb[oi][:], ALU.mult), (r_o[oi],), (r_junk,))
                    S.op("dve", I("tensor_reduce", sm_[:, 2:3], junk[:], AX.X, ALU.add), (r_junk,), (r_sm[oi],))
                    S.op("dve", I("tensor_scalar", sm_[:, 2:3], sm_[:, 2:3], 1.0 / 128, SUBLN_EPS, ALU.mult, ALU.add),
                         (r_sm[oi],), (r_sm[oi],))
                    stageB.append((oi, q0 + qs * 128))
            if gen is not None:
                drain(gen)
            flush()

    def phase_merge(l, br, wb_d, last):
        S.barrier()
        cv = Carver()
        gsb = [cv.take([512]) for _ in range(2)]
        tsb = [cv.take([512]) for _ in range(2)]
        r_g = [Reg(), Reg()]
        r_t = [Reg(), Reg()]

        def units(j):
            return [req_wb(l, br, j), req_win(l, 10240 + br * 1024 + j * 128)]
        nxt = units(0)
        cnt = 0
        for j in range(8):
            ua, ug = nxt
            if j < 7:
                nxt2 = units(j + 1)
            wa, rwa = U.ap(ua)
            for T in range(5):
                t0, t1 = TB[T]
                n = t1 - t0
                bkA, rbA = next_bank()
                for kc in range(8):
                    mm(bkA[:, 0:n], wa[:, kc, :], Yb[:, kc, t0:t1], kc == 0, kc == 7, (rwa, r_Y[kc][T]), (rbA,))
                bkG, rbG = next_bank()
                proj_fm(ug, T, bkG, rbG)
                i = cnt % 2
                cnt += 1
                S.op("act", I("activation", gsb[i][:, 0:n], bkG[:, 0:n], AF.Sigmoid),
                     (rbG,), (r_g[i],))
                if br == 0:
                    S.op("dve", I("tensor_tensor",
                        Mb[:, j, t0:t1], bkA[:, 0:n], gsb[i][:, 0:n], ALU.mult), (rbA, r_g[i]), (r_M[j][T],))
                else:
                    S.op("dve", I("tensor_tensor",
                        tsb[i][:, 0:n], bkA[:, 0:n], gsb[i][:, 0:n], ALU.mult), (rbA, r_g[i]), (r_t[i],))
                    S.op("dve", I("tensor_tensor",
                        Mb[:, j, t0:t1], tsb[i][:, 0:n], Mb[:, j, t0:t1], ALU.add), (r_t[i], r_M[j][T]), (r_M[j][T],))
            U.release(ua)
            U.release(ug)
            if j < 7:
                nxt = nxt2

    def phase_conv(l, last):
        S.barrier()
        cv = Carver()
        xi = [cv.take([512]) for _ in range(2)]
        ubuf = cv.take([SEQ + 2 + CTX + 2])
        bgz = cv.take([NT])
        ybuf = cv.take([NT])
        szblk = [cv.take([512]) for _ in range(2)]
        r_szb = [Reg(), Reg()]
        r_xi = [Reg(), Reg()]
        r_u, r_bgz, r_y, r_szc = Reg(), Reg(), Reg(), Reg()
        LOFF, COFF = 1, SEQ + 3
        S.op("pool", I("memset", ubuf[:], 0.0), (), (r_u,))

        def uoff(t0):
            return (LOFF + t0) if t0 < SEQ else (COFF + t0 - SEQ)

        def units(ch):
            return [req_win(l, (4 + c) * 1024 + ch * 128) for c in range(4)]
        nxt = units(0)
        cnt = 0
        for ch in range(8):
            uxin, ubg, ucg, uzc = nxt
            for T in range(5):
                t0, t1 = TB[T]
                n = t1 - t0
                i = cnt % 2
                cnt += 1
                bk, rb = next_bank()
                proj_fm(uxin, T, bk, rb)
                S.op("act", I("activation", xi[i][:, 0:n], bk[:, 0:n], AF.Copy),
                     (rb,), (r_xi[i],))
                bk, rb = next_bank()
                proj_fm(ucg, T, bk, rb)
                uo = uoff(t0)
                S.op("dve", I("tensor_tensor",
                    ubuf[:, uo:uo + n], bk[:, 0:n], xi[i][:, 0:n], ALU.mult), (rb, r_xi[i]), (r_u,))
                bk, rb = next_bank()
                proj_fm(uzc, T, bk, rb)
                S.op("act", I("activation", szblk[i][:, 0:n], bk[:, 0:n], AF.Silu),
                     (rb,), (r_szb[i],))
                bk, rb = next_bank()
                proj_fm(ubg, T, bk, rb)
                S.op("dve", I("tensor_tensor",
                    bgz[:, t0:t1], bk[:, 0:n], szblk[i][:, 0:n], ALU.mult), (rb, r_szb[i]), (r_bgz,))
            for u in nxt:
                U.release(u)
            if ch < 7:
                nxt = units(ch + 1)
            for (o0, y0, n) in ((LOFF, 0, SEQ), (COFF, SEQ, CTX)):
                S.op("dve", I("tensor_scalar",
                    ybuf[:, y0:y0 + n], ubuf[:, o0:o0 + n], convwT[:, 1, ch:ch + 1], None, ALU.mult),
                    (r_u, r_lay), (r_y,))
                S.op("dve", I("scalar_tensor_tensor",
                    ybuf[:, y0:y0 + n], ubuf[:, o0 - 1:o0 - 1 + n], convwT[:, 0, ch:ch + 1], ybuf[:, y0:y0 + n],
                    ALU.mult, ALU.add), (r_u, r_lay, r_y), (r_y,))
                S.op("dve", I("scalar_tensor_tensor",
                    ybuf[:, y0:y0 + n], ubuf[:, o0 + 1:o0 + 1 + n], convwT[:, 2, ch:ch + 1], ybuf[:, y0:y0 + n],
                    ALU.mult, ALU.add), (r_u, r_lay, r_y), (r_y,))
            for T in range(5):
                t0, t1 = TB[T]
                S.op("dve", I("tensor_tensor",
                    Yb[:, ch, t0:t1], ybuf[:, t0:t1], bgz[:, t0:t1], ALU.mult), (r_y, r_bgz), (r_Y[ch][T],))

    def phase_fourier(l, last):
        S.barrier()
        cv = Carver()
        ufT = cv.take([NT], BF16)
        szf = [cv.take([NT], BF16) for _ in range(2)]
        arai = [cv.take([18, 256], BF16) for _ in range(2)]
        tab = [cv.take([4, 512], BF16) for _ in range(2)]
        r_uf = Reg()
        r_szf = [Reg(), Reg()]
        r_ar = [Reg(), Reg()]
        r_tab = [Reg(), Reg()]
        sc_lat = 1.0 / math.sqrt(SEQ * 128.0)
        sc_ctx = 1.0 / math.sqrt(CTX * 128.0)

        def units(g):
            return [req_win(l, 8192 + g * 128), req_win(l, 9216 + g * 128)]
        nxt = units(0)
        tcnt = 0
        for gp in range(4):
            for gi in range(2):
                g = gp * 2 + gi
                uu, uz = nxt
                for T in range(5):
                    t0, t1 = TB[T]
                    n = t1 - t0
                    bk, rb = next_bank()
                    proj_fm(uu, T, bk, rb)
                    copy_op("dve", ufT[:, t0:t1], bk[:, 0:n], (rb,), (r_uf,))
                    bk, rb = next_bank()
                    proj_fm(uz, T, bk, rb)
                    S.op("act", I("activation", szf[gi][:, t0:t1], bk[:, 0:n], AF.Silu), (rb,), (r_szf[gi],))
                U.release(uu)
                U.release(uz)
                if g < 7:
                    nxt = units(g + 1)
                for tp in range(9):
                    bk, rb = next_bank()
                    for h2 in range(2):
                        tt = tp * 2 + h2
                        mm(bk[:, h2 * 256:(h2 + 1) * 256], ufT[:, tt * 128:(tt + 1) * 128], csc[:], True, True,
                           (r_uf, r_const), (rb,), skip=True)
                    copy_op(evac_eng(), arai[gi][:, tp * 2:tp * 2 + 2, :],
                            bk[:].rearrange("p (a b) -> p a b", a=2, b=256), (rb,), (r_ar[gi],))
            for kb in range(4):
                bks = [next_bank(), next_bank()]
                first = [True, True]
                for (tabd, off) in ((dftc_d, 0), (dfts_d, 128)):
                    for nq in range(4):
                        ti = tcnt % 2
                        tcnt += 1
                        src = tabd[nq * 512:(nq + 1) * 512, kb * 512:(kb + 1) * 512].rearrange(
                            "(t p) k -> p t k", p=128)
                        S.dma(I("dma_start", out=tab[ti][:], in_=src), "tab%d" % ti, (), (r_tab[ti],))
                        for gi in range(2):
                            bk, rb = bks[gi]
                            for n4 in range(4):
                                nt_ = nq * 4 + n4
                                lastmm = (off == 128 and nq == 3 and n4 == 3)
                                mm(bk[:], arai[gi][:, nt_, off:off + 128], tab[ti][:, n4, :], first[gi], lastmm,
                                   (r_ar[gi], r_tab[ti]), (rb,))
                                first[gi] = False
                for gi in range(2):
                    g = gp * 2 + gi
                    bk, rb = bks[gi]
                    S.op("dve", I("scalar_tensor_tensor", Yb[:, g, kb * 512:(kb + 1) * 512], bk[:], sc_lat,
                                  szf[gi][:, kb * 512:(kb + 1) * 512], ALU.mult, ALU.mult),
                         (rb, r_szf[gi]), (r_Y[g][kb],))
            for gi in range(2):
                g = gp * 2 + gi
                bk, rb = next_bank()
                first1 = True
                for (tb_, off) in ((dftcc, 0), (dftsc, 128)):
                    for nt_ in range(2):
                        mm(bk[:, 0:CTX], arai[gi][:, 16 + nt_, off:off + 128], tb_[:, nt_, :], first1,
                           (off == 128 and nt_ == 1), (r_ar[gi], r_const), (rb,))
                        first1 = False
                S.op("dve", I("scalar_tensor_tensor", Yb[:, g, SEQ:NT], bk[:, 0:CTX], sc_ctx, szf[gi][:, SEQ:NT],
                              ALU.mult, ALU.mult), (rb, r_szf[gi]), (r_Y[g][4],))

    def phase_out(b, l, last):
        S.barrier()
        cv = Carver()
        GT = cv.take([2, D])
        bgb = cv.take([D])
        srep = cv.take([16, 128])
        xt = [cv.take([D]) for _ in range(2)]
        xo = [cv.take([D]) for _ in range(2)]
        xn = cv.take([D])
        junk = cv.take([D], BF16)
        ssm = [cv.take([8]) for _ in range(2)]
        r_GT, r_bgb, r_srep = Reg(), Reg(), Reg()
        r_xt = [Reg(), Reg()]
        r_xo = [Reg(), Reg()]
        r_xn, r_junk = Reg(), Reg()
        r_ss = [Reg(), Reg()]
        uo = [req_wo(l, ob) for ob in range(8)]
        S.dma(I("dma_start", out=bgb[:], in_=bmod_d[l, 2048:3072].partition_broadcast(128)), "bgb",
              (), (r_bgb,))
        for kc in range(8):
            for s_ in range(2):
                S.op("act", I("activation", srep[:, kc * 2 + s_, :], ones_f[:], AF.Copy,
                                                                 scale=cT[:, kc, s_:s_ + 1]),
                     (r_cT, r_const), (r_srep,))
        gb = [(banks[5], r_bank[5]), (banks[6], r_bank[6]), (banks[0], r_bank[0]), (banks[1], r_bank[1])]
        for kc in range(8):
            st = kc % NST
            stv = wst[:, st].rearrange("p a b -> p (a b)")
            src = wmod_d[l, kc * 128:(kc + 1) * 128, 2048:3072]
            S.dma(I("dma_start", out=stv, in_=src), "wst%d" % st, (), (r_wst[st],))
            for s_ in range(2):
                for hf in range(2):
                    bk, rb = gb[s_ * 2 + hf]
                    mm(bk[:], srep[:, kc * 2 + s_, :], stv[:, hf * 512:(hf + 1) * 512], kc == 0, kc == 7,
                       (r_srep, r_wst[st]), (rb,))
        for s_ in range(2):
            for hf in range(2):
                bk, rb = gb[s_ * 2 + hf]
                S.op("dve", I("tensor_tensor",
                    GT[:, s_, hf * 512:(hf + 1) * 512], bk[:], bgb[:, hf * 512:(hf + 1) * 512], ALU.add),
                    (rb, r_bgb), (r_GT,))
        if not last:
            load_layer_small(l + 1)
            mods_ss(l + 1)
        ntiles = 16 if last else 18
        for t in range(ntiles):
            i = t % 2
            s_ = 0 if t < 16 else 1
            if l == 0:
                src = x_d[b, t * 128:(t + 1) * 128, :] if t < 16 else ctx_d[b, (t - 16) * 128:(t - 15) * 128, :]
            else:
                src = xs_d[t * 128:(t + 1) * 128, :]
            S.dma(I("dma_start", out=xt[i][:], in_=src), "xt%d" % i, (), (r_xt[i],))
            for hf in range(2):
                bk, rb = next_bank()
                for cb in range(4):
                    ob = hf * 4 + cb
                    wob, rwo = U.ap(uo[ob])
                    for j in range(8):
                        mm(bk[:, cb * 128:(cb + 1) * 128], Mb[:, j, t * 128:(t + 1) * 128], wob[:, j, :],
                           j == 0 and cb == 0, j == 7, (rwo, r_M[j][t // 4 if t < 16 else 4]), (rb,), skip=True)
                S.op("dve", I("tensor_tensor",
                    xo[i][:, hf * 512:(hf + 1) * 512], bk[:], GT[:, s_, hf * 512:(hf + 1) * 512], ALU.mult),
                    (rb, r_GT), (r_xo[i],))
            S.op("dve", I("tensor_tensor", xo[i][:], xo[i][:], xt[i][:], ALU.add),
                 (r_xo[i], r_xt[i]), (r_xo[i],))
            if not last:
                S.dma(I("dma_start", out=xs_d[t * 128:(t + 1) * 128, :], in_=xo[i][:]),
                      "xo%d" % i, (r_xo[i],), ())
                norm_tile(xo[i][:], r_xo[i], t, (junk[:], ssm[i][:, 0:1], xn[:], r_ss[i], r_xn))
            else:
                ss = ssm[i][:, 0:1]
                S.op("act", I("activation", junk[:], xo[i][:], AF.Square, accum_out=ss),
                     (r_xo[i],), (r_ss[i], r_junk))
                S.op("dve", I("tensor_scalar", ss, ss, 1.0 / D, NORM_EPS, ALU.mult, ALU.add),
                     (r_ss[i],), (r_ss[i],))
                S.op("act", I("activation", ss, ss, AF.Sqrt), (r_ss[i],), (r_ss[i],))
                S.op("dve", I("reciprocal", ss, ss), (r_ss[i],), (r_ss[i],))
                S.op("dve", I("scalar_tensor_tensor",
                    xo[i][:], xo[i][:], ss, fnw_b[:], ALU.mult, ALU.mult), (r_xo[i], r_ss[i], r_const), (r_xo[i],))
                S.dma(I("dma_start", out=out_d[b, t * 128:(t + 1) * 128, :], in_=xo[i][:]),
                      "xo%d" % i, (r_xo[i],), ())
        for u in uo:
            U.release(u)

    def phase_in(b):
        S.barrier()
        cv = Carver()
        xt = [cv.take([D]) for _ in range(2)]
        xn = cv.take([D])
        junk = cv.take([D], BF16)
        ssm = [cv.take([8]) for _ in range(2)]
        r_xt = [Reg(), Reg()]
        r_xn = Reg()
        r_ss = [Reg(), Reg()]
        load_c(b)
        load_layer_small(0)
        mods_ss(0)
        for t in range(18):
            i = t % 2
            src = x_d[b, t * 128:(t + 1) * 128, :] if t < 16 else ctx_d[b, (t - 16) * 128:(t - 15) * 128, :]
            S.dma(I("dma_start", out=xt[i][:], in_=src), "xt%d" % i, (), (r_xt[i],))
            norm_tile(xt[i][:], r_xt[i], t, (junk[:], ssm[i][:, 0:1], xn[:], r_ss[i], r_xn))

    for b in range(NB):
        phase_in(b)
        prefetch_attn(0)
        for l in range(NL):
            last = (l == NL - 1)
            compute_lam(l)
            phase_attn(l, last)
            prefetch_merge(l, 0)
            phase_merge(l, 0, wa_d, last)
            for c in range(4):
                req_win(l, (4 + c) * 1024, pre=True)
            phase_conv(l, last)
            prefetch_merge(l, 1)
            phase_merge(l, 1, wc_d, last)
            req_win(l, 8192, pre=True)
            req_win(l, 9216, pre=True)
            phase_fourier(l, last)
            prefetch_merge(l, 2)
            phase_merge(l, 2, wf_d, last)
            for ob in range(8):
                req_wo(l, ob, pre=True)
            phase_out(b, l, last)
            if not last:
                prefetch_attn(l + 1)

    dma_names = sorted(S.dma_cnt.keys())
    with ExitStack() as es2:
        esem = {e: es2.enter_context(nc.semaphore("s_" + e)) for e in ("pe", "act", "dve", "pool")}
        dsem = {n: es2.enter_context(nc.semaphore("d_" + n)) for n in dma_names}
        block = es2.enter_context(nc.Block())
        S.emit(nc, block, esem, dsem, final_waits=dma_names)
    es.close()
    return nc


def _tables():
    bf = ml_dtypes.bfloat16
    t = {}
    n = np.arange(SEQ, dtype=np.int64)
    nk = (n[:, None] * n[None, :]) % SEQ
    ang = 2.0 * np.pi * nk.astype(np.float64) / SEQ
    t["t_dftc"] = np.cos(ang).astype(np.float32).astype(bf)
    t["t_dfts"] = np.sin(ang).astype(np.float32).astype(bf)
    n = np.arange(CTX, dtype=np.int64)
    nk = (n[:, None] * n[None, :]) % CTX
    ang = 2.0 * np.pi * nk.astype(np.float64) / CTX
    t["t_dftc_c"] = np.cos(ang).astype(np.float32).astype(bf)
    t["t_dfts_c"] = np.sin(ang).astype(np.float32).astype(bf)
    c = np.arange(128, dtype=np.int64)
    cm = (c[:, None] * c[None, :]) % 128
    ang = 2.0 * np.pi * cm.astype(np.float64) / 128
    t["t_csc"] = np.concatenate([np.cos(ang), -np.sin(ang)], axis=1).astype(np.float32).astype(bf)
    tok = np.arange(SEQ)
    row = (tok // 64).astype(np.float32)
    col = (tok % 64).astype(np.float32)
    inv_freq = (10000.0 ** (-np.arange(0, 32, 2, dtype=np.float32) / np.float32(32))).astype(np.float32)
    ang_r = (row[:, None] * inv_freq[None, :]).astype(np.float32)
    ang_c = (col[:, None] * inv_freq[None, :]).astype(np.float32)
    cosr, sinr = np.cos(ang_r).astype(np.float32), np.sin(ang_r).astype(np.float32)
    cosc, sinc = np.cos(ang_c).astype(np.float32), np.sin(ang_c).astype(np.float32)
    C64 = np.concatenate([cosr, cosr, cosc, cosc], axis=1).T
    S64 = np.concatenate([-sinr, sinr, -sinc, sinc], axis=1).T
    t["t_ropec"] = np.ascontiguousarray(np.concatenate([C64, C64], axis=0)).astype(np.float32)
    t["t_ropes"] = np.ascontiguousarray(np.concatenate([S64, S64], axis=0)).astype(np.float32)
    t["t_identb"] = np.eye(128, dtype=np.float32).astype(bf)
    t["t_identf"] = np.eye(128, dtype=np.float32)
    P = np.zeros((128, 128), dtype=np.float32)
    for p in range(128):
        d = p % 64
        partner = p + 16 if (d % 32) < 16 else p - 16
        P[partner, p] = 1.0
    t["t_perm"] = P
    t["t_permb"] = P.astype(bf)
    return t


_CACHE = {}
import os as _os
FLAGS = {k: (_os.environ.get("KF_" + k.upper(), "1") == "1") for k in ("pipe", "defer", "accdb", "allb", "fin")}
FLAGS["rope"] = (_os.environ.get("KF_ROPE", "0") == "1")


def kernel(x, c, ctx, c_ctx, norm_w, w_mod, b_mod, w_in, lambda_qk, subln_w, conv_w,
           w_attn_o, w_conv_o, w_four_o, w_out, final_norm_w, _nb=2, _nl=DEPTH, _ncores=N_CORES):
    f = lambda a: np.ascontiguousarray(np.asarray(a, dtype=np.float32))
    x, c, ctx, c_ctx = f(x), f(c), f(ctx), f(c_ctx)
    shared = {
        "c_ctx": c_ctx, "norm_w": f(norm_w), "w_mod": f(w_mod), "b_mod": f(b_mod), "w_in": f(w_in),
        "lambda_qk": f(lambda_qk).reshape(DEPTH, 256), "subln_w": f(subln_w), "conv_w": f(conv_w),
        "w_attn_o": f(w_attn_o), "w_conv_o": f(w_conv_o), "w_four_o": f(w_four_o), "w_out": f(w_out),
        "final_norm_w": f(final_norm_w),
    }
    shared.update(_tables())
    key = (_nb, _nl)
    if key not in _CACHE:
        _CACHE[key] = build_program(_nb, _nl)
    nc = _CACHE[key]
    in_maps = []
    for i in range(_ncores):
        m = dict(shared)
        m["x"] = x[i * _nb:(i + 1) * _nb]
        m["ctx"] = ctx[i * _nb:(i + 1) * _nb]
        m["c"] = c[i * _nb:(i + 1) * _nb]
        in_maps.append(m)
    res = run_bass_kernel_spmd(nc, in_maps, core_ids=list(range(_ncores)))
    return np.concatenate([r["out"] for r in res.results], axis=0)
```

```python
import math
from contextlib import ExitStack

import numpy as np
import ml_dtypes
import concourse.bass as bass
import concourse.mybir as mybir
from concourse.bass_utils import run_bass_kernel_spmd

F32 = mybir.dt.float32
BF16 = mybir.dt.bfloat16
AF = mybir.ActivationFunctionType
ALU = mybir.AluOpType
AX = mybir.AxisListType

D = 1024
SEQ = 2048
CTX = 256
NT = SEQ + CTX
DEPTH = 4
PROJ_W = 13312
N_CORES = 8
TB = [(0, 512), (512, 1024), (1024, 1536), (1536, 2048), (2048, 2304)]
NORM_EPS = 1e-6
SUBLN_EPS = 1e-5


class Reg:
    __slots__ = ("w", "r", "name")

    def __init__(self, name=""):
        self.w = {}
        self.r = {}
        self.name = name


class Sched:
    ENG = ("pe", "act", "dve", "pool", "sp")

    def __init__(self):
        self.ops = {k: [] for k in self.ENG}
        self.pending = {k: {} for k in self.ENG}
        self.dma_cnt = {}
        self.out_tokens = {}

    @staticmethod
    def _merge(dst, src):
        for k, v in src.items():
            if dst.get(k, -1) < v:
                dst[k] = v

    def _record(self, eng, fn, reads, writes, dma_sem=None):
        deps = {}
        for r in reads:
            self._merge(deps, r.w)
        for w in writes:
            self._merge(deps, w.w)
            self._merge(deps, w.r)
        if dma_sem is not None:
            for w in writes:
                k = ("d", dma_sem)
                if k in w.w and deps.get(k) == w.w[k]:
                    deps.pop(k)
        self._merge(deps, self.pending[eng])
        self.pending[eng] = {}
        if eng == "pe":
            deps.pop(("e", "pe"), None)
        idx = len(self.ops[eng])
        for (kind, name), v in deps.items():
            if kind == "e":
                self.ops[name][v]["signal"] = True
        o = dict(fn=fn, deps=deps, signal=False, dma=dma_sem)
        self.ops[eng].append(o)
        if dma_sem is not None:
            self.dma_cnt[dma_sem] = self.dma_cnt.get(dma_sem, 0) + 16
            key, val = ("d", dma_sem), self.dma_cnt[dma_sem]
            self.out_tokens[key] = val
        else:
            key, val = ("e", eng), idx
        for r in reads:
            if r.r.get(key, -1) < val:
                r.r[key] = val
        for w in writes:
            w.w = {key: val}
            w.r = {}
        return o

    def op(self, eng, fn, reads=(), writes=()):
        return self._record(eng, fn, reads, writes)

    def dma(self, fn, sem, reads=(), writes=()):
        return self._record("sp", fn, reads, writes, dma_sem=sem)

    def barrier(self):
        deps = {}
        for e in ("pe", "act", "dve", "pool"):
            if self.ops[e]:
                deps[("e", e)] = len(self.ops[e]) - 1
        self._merge(deps, self.out_tokens)
        for e in self.ENG:
            d = dict(deps)
            d.pop(("e", e), None)
            for (kind, name), v in d.items():
                if kind == "e":
                    self.ops[name][v]["signal"] = True
            self._merge(self.pending[e], d)

    def emit(self, nc, block, esem, dsem, final_waits):
        handles = {"pe": block.tensor, "act": block.scalar, "dve": block.vector,
                   "pool": block.gpsimd, "sp": block.sync}
        cnt = {}
        for e in self.ENG:
            c = 0
            arr = []
            for o in self.ops[e]:
                if o["signal"]:
                    c += 1
                arr.append(c)
            cnt[e] = arr

        def make(e):
            ops = self.ops[e]

            def body(eng):
                waited = {}
                for o in ops:
                    for (kind, name), v in o["deps"].items():
                        if kind == "e":
                            sem, val = esem[name], cnt[name][v]
                        else:
                            sem, val = dsem[name], v
                        key = (kind, name)
                        if waited.get(key, 0) >= val:
                            continue
                        eng.wait_ge(sem, val)
                        waited[key] = val
                    ins = o["fn"](eng)
                    if o["dma"] is not None:
                        ins.then_inc(dsem[o["dma"]], 16)
                    elif o["signal"]:
                        ins.then_inc(esem[e], 1)
                if e == "sp":
                    for name in final_waits:
                        if name in self.dma_cnt:
                            eng.wait_ge(dsem[name], self.dma_cnt[name])
            return body

        for e in self.ENG:
            handles[e](make(e))


def I(method, *args, **kwargs):
    def fn(e):
        return getattr(e, method)(*args, **kwargs)
    return fn


def build_program(NB=2, NL=DEPTH):
    nc = bass.Bass("TRN2", target_bir_lowering=False)
    S = Sched()

    def din(name, shape, dt=F32):
        return nc.dram_tensor(name, list(shape), dt, kind="ExternalInput").ap()

    x_d = din("x", [NB, SEQ, D])
    ctx_d = din("ctx", [NB, CTX, D])
    c_d = din("c", [NB, D])
    cctx_d = din("c_ctx", [D])
    normw_d = din("norm_w", [DEPTH, D])
    wmod_d = din("w_mod", [DEPTH, D, 3 * D])
    bmod_d = din("b_mod", [DEPTH, 3 * D])
    win_d = din("w_in", [DEPTH, D, PROJ_W])
    lam_d = din("lambda_qk", [DEPTH, 256])
    subln_d = din("subln_w", [DEPTH, 128])
    convw_d = din("conv_w", [DEPTH, 3, D])
    wa_d = din("w_attn_o", [DEPTH, D, D])
    wc_d = din("w_conv_o", [DEPTH, D, D])
    wf_d = din("w_four_o", [DEPTH, D, D])
    wo_d = din("w_out", [DEPTH, D, D])
    fnw_d = din("final_norm_w", [D])
    dftc_d = din("t_dftc", [SEQ, SEQ], BF16)
    dfts_d = din("t_dfts", [SEQ, SEQ], BF16)
    dftcc_d = din("t_dftc_c", [CTX, CTX], BF16)
    dftsc_d = din("t_dfts_c", [CTX, CTX], BF16)
    csc_d = din("t_csc", [128, 256], BF16)
    ropec_d = din("t_ropec", [128, SEQ])
    ropes_d = din("t_ropes", [128, SEQ])
    identb_d = din("t_identb", [128, 128], BF16)
    identf_d = din("t_identf", [128, 128])
    perm_d = din("t_perm", [128, 128])
    permb_d = din("t_permb", [128, 128], BF16)
    xs_d = nc.dram_tensor("xs", [NT, D], F32, kind="Internal").ap()
    out_d = nc.dram_tensor("out", [NB, SEQ, D], F32, kind="ExternalOutput").ap()

    es = ExitStack()

    def sb(name, shape, dt=F32):
        return es.enter_context(nc.sbuf_tensor(name, list(shape), dt))

    def psum(name, shape, dt=F32):
        return es.enter_context(nc.psum_tensor(name, list(shape), dt))

    hT = sb("hT", [128, 8, NT], BF16)
    Yb = sb("Yb", [128, 8, NT], BF16)
    Mb = sb("Mb", [128, 8, NT], BF16)
    r_hT = [Reg("hT%d" % i) for i in range(5)]
    r_Y = [[Reg() for _ in range(5)] for _ in range(8)]
    r_M = [[Reg() for _ in range(5)] for _ in range(8)]

    identb = sb("identb", [128, 128], BF16)
    identf = sb("identf", [128, 128])
    perm = sb("perm", [128, 128])
    permb = sb("permb", [128, 128], BF16)
    ropec = sb("ropec", [128, SEQ])
    ropes = sb("ropes", [128, SEQ])
    csc = sb("csc", [128, 256], BF16)
    dftcc = sb("dftcc", [128, 2, CTX], BF16)
    dftsc = sb("dftsc", [128, 2, CTX], BF16)
    ones_f = sb("ones_f", [128, 128])
    fnw_b = sb("fnw_b", [128, D])
    r_const = Reg("const")

    cT = sb("cT", [128, 8, 2])
    r_cT = Reg("cT")
    normwT = sb("normwT", [128, 8])
    bmodT = sb("bmodT", [128, 24])
    convwT = sb("convwT", [128, 3, 8])
    sublnb = sb("sublnb", [128, 128])
    lamb = sb("lamb", [128, 256])
    r_lay = Reg("lay")
    gam = sb("gam", [128, 8, 2])
    bet = sb("bet", [128, 8, 2])
    r_gb = Reg("gb")
    nlam = sb("nlam", [128, 1])
    lamtmp = sb("lamtmp", [128, 2, 64])
    lam2 = sb("lam2", [128, 2])
    r_lam = Reg("lam")

    NST, NUS = 3, 8
    wst = sb("wst", [128, NST, 8, 128])
    r_wst = [Reg("wst%d" % i) for i in range(NST)]
    wub = sb("wub", [128, NUS, 8, 128], BF16)
    r_wub = [Reg("wub%d" % i) for i in range(NUS)]

    ARENA = 43 * 1024
    arena = sb("arena", [128, ARENA // 4])

    class Carver:
        def __init__(self):
            self.off = 0

        def take(self, shape, dt=F32):
            n = 1
            for s_ in shape:
                n *= s_
            esz = 4 if dt == F32 else 2
            nbytes = (n * esz + 31) // 32 * 32
            assert self.off + nbytes <= ARENA, ("arena overflow", self.off, nbytes)
            w0 = self.off // 4
            ap = arena[:, w0:w0 + nbytes // 4]
            if dt != F32:
                ap = ap.bitcast(dt)
            ap = ap[:, 0:n]
            if len(shape) == 2:
                pat = "p (a b) -> p a b"
                ap = ap.rearrange(pat, a=shape[0], b=shape[1])
            elif len(shape) == 3:
                ap = ap.rearrange("p (a b c) -> p a b c", a=shape[0], b=shape[1], c=shape[2])
            self.off += nbytes
            return ap

    banks = [psum("pb%d" % i, [128, 512]) for i in range(7)]
    r_bank = [Reg("bank%d" % i) for i in range(7)]
    pst = psum("pst", [128, 1024], BF16)
    r_pst = [Reg("pst%d" % i) for i in range(4)]
    rot = {}

    def next_bank(pool=(0, 1, 2, 3, 4)):
        k = rot.get(pool, 0)
        rot[pool] = (k + 1) % len(pool)
        i = pool[k]
        return banks[i], r_bank[i]

    flip = {"i": 0}

    def evac_eng():
        flip["i"] ^= 1
        return "act" if flip["i"] else "dve"

    def copy_op(eng, out, in_, reads, writes):
        if eng == "act":
            S.op("act", I("activation", out, in_, AF.Copy), reads, writes)
        elif eng == "dve":
            S.op("dve", I("tensor_copy", out, in_), reads, writes)
        else:
            S.op("pool", I("tensor_copy", out, in_), reads, writes)

    def mm(out, lhsT, rhs, start, stop, reads, writes, skip=False):
        S.op("pe", I("matmul", out, lhsT, rhs, start=start, stop=stop, skip_group_check=skip),
             reads, writes)

    class Units:
        def __init__(self):
            self.n = 0
            self.released = {}
            self.pref = {}

        def prefetch(self, key, src_ap):
            if key not in self.pref:
                self.pref[key] = self.request(src_ap)

        def request(self, src_ap, key=None):
            if key is not None and key in self.pref:
                return self.pref.pop(key)
            u = self.n
            self.n += 1
            if u >= NUS:
                assert self.released.get(u - NUS, False), "unit slot reuse before release"
            st = u % NST
            sl = u % NUS
            src = src_ap.rearrange("(kc p) c -> p kc c", p=128)
            S.dma(I("dma_start", out=wst[:, st], in_=src), "wst%d" % st,
                  reads=(), writes=(r_wst[st],))
            S.op("pool", I("tensor_copy", wub[:, sl], wst[:, st]),
                 reads=(r_wst[st],), writes=(r_wub[sl],))
            return u

        def ap(self, u):
            return wub[:, u % NUS], r_wub[u % NUS]

        def release(self, u):
            self.released[u] = True

    U = Units()

    def win_cols(l, c0):
        return win_d[l, :, c0:c0 + 128]

    def req_win(l, c0, pre=False):
        key = ("win", l, c0)
        if pre:
            return U.prefetch(key, win_cols(l, c0))
        return U.request(win_cols(l, c0), key)

    WB = {0: wa_d, 1: wc_d, 2: wf_d}

    def req_wb(l, br, j, pre=False):
        key = ("wb", l, br, j)
        ap = WB[br][l, :, j * 128:(j + 1) * 128]
        if pre:
            return U.prefetch(key, ap)
        return U.request(ap, key)

    def req_wo(l, ob, pre=False):
        key = ("wo", l, ob)
        ap = wo_d[l, :, ob * 128:(ob + 1) * 128]
        if pre:
            return U.prefetch(key, ap)
        return U.request(ap, key)

    def prefetch_attn(l):
        for c in range(4):
            req_win(l, c * 1024, pre=True)

    def prefetch_merge(l, br):
        req_wb(l, br, 0, pre=True)
        req_win(l, 10240 + br * 1024, pre=True)

    def cdma(out, in_, **kw):
        S.dma(I("dma_start", out=out, in_=in_, **kw), "const", reads=(), writes=(r_const,))

    cdma(identb[:], identb_d)
    cdma(identf[:], identf_d)
    cdma(perm[:], perm_d)
    cdma(permb[:], permb_d)
    cdma(ropec[:], ropec_d)
    cdma(ropes[:], ropes_d)
    cdma(csc[:], csc_d)
    cdma(dftcc[:], dftcc_d.rearrange("(t p) k -> p t k", p=128))
    cdma(dftsc[:], dftsc_d.rearrange("(t p) k -> p t k", p=128))
    cdma(fnw_b[:], fnw_d.partition_broadcast(128))
    S.op("dve", I("memset", ones_f[:], 1.0), (), (r_const,))

    def load_layer_small(l):
        def d(out, in_, **kw):
            S.dma(I("dma_start", out=out, in_=in_, **kw), "lay", reads=(), writes=(r_lay,))
        d(normwT[:], normw_d[l].rearrange("(j p) -> p j", p=128), allow_slow_non_contiguous=True)
        d(bmodT[:], bmod_d[l].rearrange("(j p) -> p j", p=128), allow_slow_non_contiguous=True)
        d(convwT[:], convw_d[l].rearrange("k (c p) -> p k c", p=128), allow_slow_non_contiguous=True)
        d(sublnb[:], subln_d[l].partition_broadcast(128))
        d(lamb[:], lam_d[l].partition_broadcast(128))

    def compute_lam(l):
        lam_init = 0.8 - 0.6 * math.exp(-0.3 * l)
        lv = lamb[:].rearrange("p (a b d) -> p a b d", a=2, b=2, d=64)
        S.op("dve", I("tensor_tensor", lamtmp[:], lv[:, :, 0, :], lv[:, :, 1, :], ALU.mult),
             (r_lay,), (r_lam,))
        S.op("dve", I("tensor_reduce", lam2[:], lamtmp[:], AX.X, ALU.add), (r_lam,), (r_lam,))
        S.op("act", I("activation", lam2[:], lam2[:], AF.Exp), (r_lam,), (r_lam,))
        S.op("dve", I("tensor_tensor", nlam[:], lam2[:, 1:2], lam2[:, 0:1], ALU.subtract),
             (r_lam,), (r_lam,))
        S.op("dve", I("tensor_scalar", nlam[:], nlam[:], -lam_init, None, ALU.add),
             (r_lam,), (r_lam,))
        S.op("dve", I("tensor_scalar", sublnb[:], sublnb[:], 1.0 - lam_init, None, ALU.mult),
             (r_lay,), (r_lay,))

    def load_c(b):
        S.dma(I("dma_start", out=cT[:, :, 0], in_=c_d[b].rearrange("(j p) -> p j", p=128),
                                    allow_slow_non_contiguous=True), "cload", (), (r_cT,))
        S.dma(I("dma_start", out=cT[:, :, 1], in_=cctx_d.rearrange("(j p) -> p j", p=128),
                                    allow_slow_non_contiguous=True), "cload", (), (r_cT,))
        S.op("act", I("activation", cT[:], cT[:], AF.Silu), (r_cT,), (r_cT,))

    def mods_ss(l):
        bk, rb = banks[5], r_bank[5]
        first = True
        for kc in range(8):
            for half in range(2):
                st = (kc * 2 + half) % NST
                stv = wst[:, st].rearrange("p a b -> p (a b)")
                src = wmod_d[l, kc * 128:(kc + 1) * 128, half * 1024:(half + 1) * 1024]
                S.dma(I("dma_start", out=stv, in_=src), "wst%d" % st, (), (r_wst[st],))
                for cc in range(8):
                    col = (half * 8 + cc) * 2
                    o_ = bk[:, col:col + 2]
                    l_ = stv[:, cc * 128:(cc + 1) * 128]
                    r_ = cT[:, kc, :]
                    mm(o_, l_, r_, first, kc == 7, (r_wst[st], r_cT), (rb,), skip=True)
                    first = False
        pv = bk[:, 0:32].rearrange("p (h c s) -> p h c s", h=2, c=8, s=2)
        bm = bmodT[:]
        for s_ in range(2):
            S.op("dve", I("tensor_tensor", bet[:, :, s_], pv[:, 0, :, s_], bm[:, 0:8], ALU.add),
                 (rb, r_lay), (r_gb,))
            S.op("dve", I("scalar_tensor_tensor", gam[:, :, s_], pv[:, 1, :, s_], 1.0, bm[:, 8:16],
                                                                 ALU.add, ALU.add), (rb, r_lay), (r_gb,))
            S.op("dve", I("tensor_tensor", gam[:, :, s_], gam[:, :, s_], normwT[:], ALU.mult),
                 (r_gb, r_lay), (r_gb,))

    def norm_tile(xsrc, r_x, t, scr):
        s_ = 0 if t < 16 else 1
        blk = t // 4 if t < 16 else 4
        junk, ss, xn, r_ss, r_xn = scr
        S.op("act", I("activation", junk, xsrc, AF.Square, accum_out=ss), (r_x,), (r_ss,))
        S.op("dve", I("tensor_scalar", ss, ss, 1.0 / D, NORM_EPS, ALU.mult, ALU.add), (r_ss,), (r_ss,))
        S.op("act", I("activation", ss, ss, AF.Sqrt), (r_ss,), (r_ss,))
        S.op("dve", I("reciprocal", ss, ss), (r_ss,), (r_ss,))
        S.op("act", I("activation", xn, xsrc, AF.Copy, scale=ss), (r_x, r_ss), (r_xn,))
        for hf in range(2):
            bk, rb = next_bank()
            for jj in range(4):
                j = hf * 4 + jj
                S.op("pe", I("transpose", bk[:, jj * 128:(jj + 1) * 128],
                                                                    xn[:, j * 128:(j + 1) * 128], identf[:]),
                     (r_xn, r_const), (rb,))
            for jj in range(4):
                j = hf * 4 + jj
                S.op("act", I("activation",
                    hT[:, j, t * 128:(t + 1) * 128], bk[:, jj * 128:(jj + 1) * 128], AF.Identity,
                    bias=bet[:, j, s_:s_ + 1], scale=gam[:, j, s_:s_ + 1]),
                    (rb, r_gb), (r_hT[blk],))

    def proj_fm(u, T, bk, rb):
        wap, rw = U.ap(u)
        t0, t1 = TB[T]
        n = t1 - t0
        for kc in range(8):
            mm(bk[:, 0:n], wap[:, kc, :], hT[:, kc, t0:t1], kc == 0, kc == 7, (rw, r_hT[T]), (rb,))
        return n

    def phase_attn(l, last):
        S.barrier()
        cv = Carver()
        qT = cv.take([NT], BF16)
        kT0 = cv.take([NT], BF16)
        kT1 = cv.take([NT], BF16)
        szT = cv.take([NT], BF16)
        vaug = cv.take([18, 129], BF16)
        qf = [cv.take([512]) for _ in range(2)]
        t1b = [cv.take([512]) for _ in range(2)]
        Et = [cv.take([512], BF16) for _ in range(3)]
        NSLOT = 6
        osb = [cv.take([128]) for _ in range(NSLOT)]
        ybf = [cv.take([128], BF16) for _ in range(NSLOT)]
        sm = [cv.take([8]) for _ in range(NSLOT)]
        junk = cv.take([128])
        mhalf = cv.take([8])
        r_q, r_k, r_sz, r_v = Reg(), Reg(), Reg(), Reg()
        r_qf = [Reg(), Reg()]
        r_t1 = [Reg(), Reg()]
        r_E = [Reg(), Reg(), Reg()]
        r_o = [Reg() for _ in range(6)]
        r_yb = [Reg() for _ in range(6)]
        r_sm = [Reg() for _ in range(6)]
        r_junk, r_mh = Reg(), Reg()
        S.op("pool", I("memset", vaug[:, :, 128:129], 1.0), (), (r_v,))
        S.op("pool", I("memset", kT0[64:128, :], 0.0), (), (r_k,))
        S.op("pool", I("memset", kT1[0:64, :], 0.0), (), (r_k,))
        ALLB = (0, 1, 2, 3, 4, 5, 6) if FLAGS["allb"] else (0, 1, 2, 3, 4)
        SCB = (0, 1, 2) if FLAGS["allb"] else (0, 1, 2, 3, 4)
        ACC = ((3, 4), (5, 6)) if FLAGS["allb"] else ((5, 6), (5, 6))

        def head_units(hd):
            return [req_win(l, c * 1024 + hd * 128) for c in range(4)]

        qhi = [Mb[:, 6, 0:512], Mb[:, 6, 512:1024]]
        qlo = [Mb[:, 6, 1024:1536], Mb[:, 6, 1536:2048]]
        r_qh = [Reg(), Reg()]

        def rope_tail(i, dst, rd, t0, t1, n):
            bk2, rb2 = next_bank(ALLB)
            if FLAGS["rope"]:
                mm(bk2[:, 0:n], permb[:], qhi[i][:, 0:n], True, False, (r_qh[i], r_const), (rb2,))
                mm(bk2[:, 0:n], permb[:], qlo[i][:, 0:n], False, True, (r_qh[i], r_const), (rb2,))
            else:
                mm(bk2[:, 0:n], perm[:], qf[i][:, 0:n], True, True, (r_qf[i], r_const), (rb2,))
            S.op("dve", I("tensor_tensor", qf[i][:, 0:n], bk2[:, 0:n], ropes[:, t0:t1], ALU.mult),
                 (rb2, r_const, r_qf[i]), (r_qf[i],))
            if dst is not None:
                S.op("pool", I("tensor_tensor", dst[:, t0:t1], t1b[i][:, 0:n], qf[i][:, 0:n], ALU.add),
                     (r_t1[i], r_qf[i]), (rd,))
            else:
                S.op("pool", I("tensor_tensor", kT0[0:64, t0:t1], t1b[i][0:64, 0:n], qf[i][0:64, 0:n],
                               ALU.add), (r_t1[i], r_qf[i]), (rd,))
                S.op("pool", I("tensor_tensor", kT1[64:128, t0:t1], t1b[i][64:128, 0:n],
                               qf[i][64:128, 0:n], ALU.add), (r_t1[i], r_qf[i]), (rd,))

        nxt = head_units(0)
        ctr = {"qf": 0, "E": 0, "o": 0, "qb": 0}
        for hd in range(8):
            uq, uk, uv, uz = nxt
            for T in range(5):
                t0, t1 = TB[T]
                n = t1 - t0
                ropes_todo = []
                for (u, dst, rd) in ((uq, qT, r_q), (uk, None, r_k)):
                    bk, rb = next_bank(ALLB)
                    proj_fm(u, T, bk, rb)
                    if T < 4:
                        i = ctr["qf"] % 2
                        ctr["qf"] += 1
                        if FLAGS["rope"]:
                            S.op("act", I("activation", qhi[i][:, 0:n], bk[:, 0:n], AF.Copy), (rb,), (r_qh[i],))
                            S.op("dve", I("tensor_tensor", qlo[i][:, 0:n], bk[:, 0:n], qhi[i][:, 0:n], ALU.subtract),
                                 (rb, r_qh[i]), (r_qh[i],))
                            S.op("dve", I("tensor_tensor", t1b[i][:, 0:n], bk[:, 0:n], ropec[:, t0:t1], ALU.mult),
                                 (rb, r_const), (r_t1[i],))
                            ropes_todo.append((i, dst, rd))
                        else:
                            S.op("act", I("activation", qf[i][:, 0:n], bk[:, 0:n], AF.Copy), (rb,), (r_qf[i],))
                            S.op("dve", I("tensor_tensor", t1b[i][:, 0:n], qf[i][:, 0:n], ropec[:, t0:t1], ALU.mult),
                                 (r_qf[i], r_const), (r_t1[i],))
                            rope_tail(i, dst, rd, t0, t1, n)
                    else:
                        if dst is not None:
                            copy_op("dve", dst[:, t0:t1], bk[:, 0:n], (rb,), (rd,))
                        else:
                            copy_op("dve", kT0[0:64, t0:t1], bk[0:64, 0:n], (rb,), (rd,))
                            copy_op("dve", kT1[64:128, t0:t1], bk[64:128, 0:n], (rb,), (rd,))
                bk, rb = next_bank(ALLB)
                proj_fm(uz, T, bk, rb)
                S.op("act", I("activation", szT[:, t0:t1], bk[:, 0:n], AF.Silu), (rb,), (r_sz,))
                wv, rwv = U.ap(uv)
                ntile = n // 128
                bk, rb = next_bank(ALLB)
                for tt in range(ntile):
                    tok0 = t0 + tt * 128
                    for kc in range(8):
                        mm(bk[:, tt * 128:(tt + 1) * 128], hT[:, kc, tok0:tok0 + 128], wv[:, kc, :],
                           kc == 0 and tt == 0, kc == 7, (rwv, r_hT[T]), (rb,), skip=True)
                tile0 = t0 // 128
                S.op("dve", I("tensor_copy", vaug[:, tile0:tile0 + ntile, 0:128],
                              bk[:, 0:ntile * 128].rearrange("p (a b) -> p a b", a=ntile, b=128)), (rb,), (r_v,))
                for (i, dst, rd) in ropes_todo:
                    rope_tail(i, dst, rd, t0, t1, n)
            for u in nxt:
                U.release(u)
            if hd < 7:
                nxt = head_units(hd + 1)

            qblocks = [(q0, list(range(18))) for q0 in range(0, SEQ, 256)]
            if not last:
                qblocks.append((SEQ, [16, 17]))
            stageB = []

            def flush():
                for (oi, qa) in stageB:
                    sm_ = sm[oi]
                    S.op("act", I("activation", sm_[:, 3:4], sm_[:, 2:3], AF.Sqrt), (r_sm[oi],), (r_sm[oi],))
                for (oi, qa) in stageB:
                    sm_ = sm[oi]
                    S.op("dve", I("reciprocal", sm_[:, 3:4], sm_[:, 3:4]), (r_sm[oi],), (r_sm[oi],))
                    S.op("dve", I("scalar_tensor_tensor", ybf[oi][:], osb[oi][:], sm_[:, 3:4], sublnb[:],
                                  ALU.mult, ALU.mult), (r_o[oi], r_sm[oi], r_lay), (r_yb[oi],))
                    pcol = (oi % 4) * 128
                    S.op("pe", I("transpose", pst[:, pcol:pcol + 128], ybf[oi][:], identb[:]),
                         (r_yb[oi], r_const), (r_pst[oi % 4],))
                    blk = qa // 512 if qa < SEQ else 4
                    S.op("dve", I("tensor_tensor", Yb[:, hd, qa:qa + 128], pst[:, pcol:pcol + 128],
                                  szT[:, qa:qa + 128], ALU.mult), (r_pst[oi % 4], r_sz), (r_Y[hd][blk],))
                del stageB[:]

            def scores(q0, kt):
                bk, rb = next_bank(SCB)
                for m in range(2):
                    kTm = kT0 if m == 0 else kT1
                    mm(bk[:, m * 256:(m + 1) * 256], kTm[:, kt * 128:(kt + 1) * 128],
                       qT[:, q0:q0 + 256], True, True, (r_q, r_k), (rb,), skip=True)
                return bk, rb

            steps = [(bi, ki) for bi, (q0_, kts_) in enumerate(qblocks) for ki in range(len(kts_))]
            LOOK = 2
            sc = {}

            def issue(idx):
                bi_, ki_ = steps[idx]
                sc[idx] = scores(qblocks[bi_][0], qblocks[bi_][1][ki_])

            for idx in range(min(LOOK, len(steps))):
                issue(idx)
            for idx, (bi, ki) in enumerate(steps):
                q0, kts = qblocks[bi]
                kt = kts[ki]
                if ki == 0:
                    par = (ctr["qb"] % 2) if FLAGS["accdb"] else 0
                    ctr["qb"] += 1
                    accs = [(banks[ACC[par][0]], r_bank[ACC[par][0]]), (banks[ACC[par][1]], r_bank[ACC[par][1]])]
                if idx + LOOK < len(steps):
                    issue(idx + LOOK)
                bk, rb = sc.pop(idx)
                ei = ctr["E"] % 3
                ctr["E"] += 1
                S.op("act", I("activation", Et[ei][:], bk[:], AF.Exp, scale=0.125), (rb,), (r_E[ei],))
                for qs in range(2):
                    ab, rab = accs[qs]
                    for m in range(2):
                        mm(ab[:, m * 256:m * 256 + 129],
                           Et[ei][:, m * 256 + qs * 128:m * 256 + (qs + 1) * 128], vaug[:, kt, :],
                           ki == 0 and m == 0, ki == len(kts) - 1, (r_E[ei], r_v), (rab,), skip=True)
                if ki != len(kts) - 1:
                    continue
                flush()
                for qs in range(2):
                    ab, rab = accs[qs]
                    oi = ctr["o"] % NSLOT
                    ctr["o"] += 1
                    sm_ = sm[oi]
                    den = ab[:, 0:512].rearrange("p (m c) -> p m c", m=2, c=256)[:, :, 128]
                    S.op("dve", I("reciprocal", sm_[:, 0:2], den), (rab,), (r_sm[oi],))
                    S.op("dve", I("tensor_tensor", sm_[:, 1:2], sm_[:, 1:2], nlam[:], ALU.mult),
                         (r_sm[oi], r_lam), (r_sm[oi],))
                    if FLAGS["fin"]:
                        S.op("dve", I("tensor_scalar", osb[oi][:], ab[:, 0:128], sm_[:, 0:1], None, ALU.mult),
                             (rab, r_sm[oi]), (r_o[oi],))
                    else:
                        S.op("act", I("activation", osb[oi][:], ab[:, 0:128], AF.Copy, scale=sm_[:, 0:1]),
                             (rab, r_sm[oi]), (r_o[oi],))
                    S.op("dve", I("scalar_tensor_tensor", osb[oi][:], ab[:, 256:384], sm_[:, 1:2], osb[oi][:],
                                  ALU.mult, ALU.add), (rab, r_sm[oi], r_o[oi]), (r_o[oi],))
                    if FLAGS["fin"]:
                        S.op("dve", I("tensor_tensor", junk[:], osb[oi][:], osb[oi][:], ALU.mult), (r_o[oi],), (r_junk,))
                        S.op("dve", I("tensor_reduce", sm_[:, 2:3], junk[:], AX.X, ALU.add), (r_junk,), (r_sm[oi],))
                    else:
                        S.op("act", I("activation", junk[:], osb[oi][:], AF.Square, accum_out=sm_[:, 2:3]),
                             (r_o[oi],), (r_sm[oi], r_junk))
                    S.op("dve", I("tensor_scalar", sm_[:, 2:3], sm_[:, 2:3], 1.0 / 128, SUBLN_EPS, ALU.mult, ALU.add),
                         (r_sm[oi],), (r_sm[oi],))
                    stageB.append((oi, q0 + qs * 128))
                if not FLAGS["defer"]:
                    flush()
            flush()

    def phase_merge(l, br, wb_d, last):
        S.barrier()
        cv = Carver()
        gsb = [cv.take([512]) for _ in range(2)]
        tsb = [cv.take([512]) for _ in range(2)]
        r_g = [Reg(), Reg()]
        r_t = [Reg(), Reg()]

        def units(j):
            return [req_wb(l, br, j), req_win(l, 10240 + br * 1024 + j * 128)]
        nxt = units(0)
        cnt = 0
        for j in range(8):
            ua, ug = nxt
            if j < 7:
                nxt2 = units(j + 1)
            wa, rwa = U.ap(ua)
            for T in range(5):
                t0, t1 = TB[T]
                n = t1 - t0
                bkA, rbA = next_bank()
                for kc in range(8):
                    mm(bkA[:, 0:n], wa[:, kc, :], Yb[:, kc, t0:t1], kc == 0, kc == 7, (rwa, r_Y[kc][T]), (rbA,))
                bkG, rbG = next_bank()
                proj_fm(ug, T, bkG, rbG)
                i = cnt % 2
                cnt += 1
                S.op("act", I("activation", gsb[i][:, 0:n], bkG[:, 0:n], AF.Sigmoid),
                     (rbG,), (r_g[i],))
                if br == 0:
                    S.op("dve", I("tensor_tensor",
                        Mb[:, j, t0:t1], bkA[:, 0:n], gsb[i][:, 0:n], ALU.mult), (rbA, r_g[i]), (r_M[j][T],))
                else:
                    S.op("dve", I("tensor_tensor",
                        tsb[i][:, 0:n], bkA[:, 0:n], gsb[i][:, 0:n], ALU.mult), (rbA, r_g[i]), (r_t[i],))
                    S.op("dve", I("tensor_tensor",
                        Mb[:, j, t0:t1], tsb[i][:, 0:n], Mb[:, j, t0:t1], ALU.add), (r_t[i], r_M[j][T]), (r_M[j][T],))
            U.release(ua)
            U.release(ug)
            if j < 7:
                nxt = nxt2

    def phase_conv(l, last):
        S.barrier()
        cv = Carver()
        xi = [cv.take([512]) for _ in range(2)]
        ubuf = cv.take([SEQ + 2 + CTX + 2])
        bgz = cv.take([NT])
        ybuf = cv.take([NT])
        szblk = [cv.take([512]) for _ in range(2)]
        r_szb = [Reg(), Reg()]
        r_xi = [Reg(), Reg()]
        r_u, r_bgz, r_y, r_szc = Reg(), Reg(), Reg(), Reg()
        LOFF, COFF = 1, SEQ + 3
        S.op("pool", I("memset", ubuf[:], 0.0), (), (r_u,))

        def uoff(t0):
            return (LOFF + t0) if t0 < SEQ else (COFF + t0 - SEQ)

        def units(ch):
            return [req_win(l, (4 + c) * 1024 + ch * 128) for c in range(4)]
        nxt = units(0)
        cnt = 0
        for ch in range(8):
            uxin, ubg, ucg, uzc = nxt
            for T in range(5):
                t0, t1 = TB[T]
                n = t1 - t0
                i = cnt % 2
                cnt += 1
                bk, rb = next_bank()
                proj_fm(uxin, T, bk, rb)
                S.op("act", I("activation", xi[i][:, 0:n], bk[:, 0:n], AF.Copy),
                     (rb,), (r_xi[i],))
                bk, rb = next_bank()
                proj_fm(ucg, T, bk, rb)
                uo = uoff(t0)
                S.op("dve", I("tensor_tensor",
                    ubuf[:, uo:uo + n], bk[:, 0:n], xi[i][:, 0:n], ALU.mult), (rb, r_xi[i]), (r_u,))
                bk, rb = next_bank()
                proj_fm(uzc, T, bk, rb)
                S.op("act", I("activation", szblk[i][:, 0:n], bk[:, 0:n], AF.Silu),
                     (rb,), (r_szb[i],))
                bk, rb = next_bank()
                proj_fm(ubg, T, bk, rb)
                S.op("dve", I("tensor_tensor",
                    bgz[:, t0:t1], bk[:, 0:n], szblk[i][:, 0:n], ALU.mult), (rb, r_szb[i]), (r_bgz,))
            for u in nxt:
                U.release(u)
            if ch < 7:
                nxt = units(ch + 1)
            for (o0, y0, n) in ((LOFF, 0, SEQ), (COFF, SEQ, CTX)):
                S.op("dve", I("tensor_scalar",
                    ybuf[:, y0:y0 + n], ubuf[:, o0:o0 + n], convwT[:, 1, ch:ch + 1], None, ALU.mult),
                    (r_u, r_lay), (r_y,))
                S.op("dve", I("scalar_tensor_tensor",
                    ybuf[:, y0:y0 + n], ubuf[:, o0 - 1:o0 - 1 + n], convwT[:, 0, ch:ch + 1], ybuf[:, y0:y0 + n],
                    ALU.mult, ALU.add), (r_u, r_lay, r_y), (r_y,))
                S.op("dve", I("scalar_tensor_tensor",
                    ybuf[:, y0:y0 + n], ubuf[:, o0 + 1:o0 + 1 + n], convwT[:, 2, ch:ch + 1], ybuf[:, y0:y0 + n],
                    ALU.mult, ALU.add), (r_u, r_lay, r_y), (r_y,))
            for T in range(5):
                t0, t1 = TB[T]
                S.op("dve", I("tensor_tensor",
                    Yb[:, ch, t0:t1], ybuf[:, t0:t1], bgz[:, t0:t1], ALU.mult), (r_y, r_bgz), (r_Y[ch][T],))

    def phase_fourier(l, last):
        S.barrier()
        cv = Carver()
        ufT = cv.take([NT], BF16)
        szf = [cv.take([NT], BF16) for _ in range(2)]
        arai = [cv.take([18, 256], BF16) for _ in range(2)]
        tab = [cv.take([4, 512], BF16) for _ in range(2)]
        r_uf = Reg()
        r_szf = [Reg(), Reg()]
        r_ar = [Reg(), Reg()]
        r_tab = [Reg(), Reg()]
        sc_lat = 1.0 / math.sqrt(SEQ * 128.0)
        sc_ctx = 1.0 / math.sqrt(CTX * 128.0)

        def units(g):
            return [req_win(l, 8192 + g * 128), req_win(l, 9216 + g * 128)]
        nxt = units(0)
        tcnt = 0
        for gp in range(4):
            for gi in range(2):
                g = gp * 2 + gi
                uu, uz = nxt
                for T in range(5):
                    t0, t1 = TB[T]
                    n = t1 - t0
                    bk, rb = next_bank()
                    proj_fm(uu, T, bk, rb)
                    copy_op("dve", ufT[:, t0:t1], bk[:, 0:n], (rb,), (r_uf,))
                    bk, rb = next_bank()
                    proj_fm(uz, T, bk, rb)
                    S.op("act", I("activation", szf[gi][:, t0:t1], bk[:, 0:n], AF.Silu), (rb,), (r_szf[gi],))
                U.release(uu)
                U.release(uz)
                if g < 7:
                    nxt = units(g + 1)
                for tp in range(9):
                    bk, rb = next_bank()
                    for h2 in range(2):
                        tt = tp * 2 + h2
                        mm(bk[:, h2 * 256:(h2 + 1) * 256], ufT[:, tt * 128:(tt + 1) * 128], csc[:], True, True,
                           (r_uf, r_const), (rb,), skip=True)
                    copy_op(evac_eng(), arai[gi][:, tp * 2:tp * 2 + 2, :],
                            bk[:].rearrange("p (a b) -> p a b", a=2, b=256), (rb,), (r_ar[gi],))
            for kb in range(4):
                bks = [next_bank(), next_bank()]
                first = [True, True]
                for (tabd, off) in ((dftc_d, 0), (dfts_d, 128)):
                    for nq in range(4):
                        ti = tcnt % 2
                        tcnt += 1
                        src = tabd[nq * 512:(nq + 1) * 512, kb * 512:(kb + 1) * 512].rearrange(
                            "(t p) k -> p t k", p=128)
                        S.dma(I("dma_start", out=tab[ti][:], in_=src), "tab%d" % ti, (), (r_tab[ti],))
                        for gi in range(2):
                            bk, rb = bks[gi]
                            for n4 in range(4):
                                nt_ = nq * 4 + n4
                                lastmm = (off == 128 and nq == 3 and n4 == 3)
                                mm(bk[:], arai[gi][:, nt_, off:off + 128], tab[ti][:, n4, :], first[gi], lastmm,
                                   (r_ar[gi], r_tab[ti]), (rb,))
                                first[gi] = False
                for gi in range(2):
                    g = gp * 2 + gi
                    bk, rb = bks[gi]
                    S.op("dve", I("scalar_tensor_tensor", Yb[:, g, kb * 512:(kb + 1) * 512], bk[:], sc_lat,
                                  szf[gi][:, kb * 512:(kb + 1) * 512], ALU.mult, ALU.mult),
                         (rb, r_szf[gi]), (r_Y[g][kb],))
            for gi in range(2):
                g = gp * 2 + gi
                bk, rb = next_bank()
                first1 = True
                for (tb_, off) in ((dftcc, 0), (dftsc, 128)):
                    for nt_ in range(2):
                        mm(bk[:, 0:CTX], arai[gi][:, 16 + nt_, off:off + 128], tb_[:, nt_, :], first1,
                           (off == 128 and nt_ == 1), (r_ar[gi], r_const), (rb,))
                        first1 = False
                S.op("dve", I("scalar_tensor_tensor", Yb[:, g, SEQ:NT], bk[:, 0:CTX], sc_ctx, szf[gi][:, SEQ:NT],
                              ALU.mult, ALU.mult), (rb, r_szf[gi]), (r_Y[g][4],))

    def phase_out(b, l, last):
        S.barrier()
        cv = Carver()
        GT = cv.take([2, D])
        bgb = cv.take([D])
        srep = cv.take([16, 128])
        xt = [cv.take([D]) for _ in range(2)]
        xo = [cv.take([D]) for _ in range(2)]
        xn = cv.take([D])
        junk = cv.take([D], BF16)
        ssm = [cv.take([8]) for _ in range(2)]
        r_GT, r_bgb, r_srep = Reg(), Reg(), Reg()
        r_xt = [Reg(), Reg()]
        r_xo = [Reg(), Reg()]
        r_xn, r_junk = Reg(), Reg()
        r_ss = [Reg(), Reg()]
        uo = [req_wo(l, ob) for ob in range(8)]
        S.dma(I("dma_start", out=bgb[:], in_=bmod_d[l, 2048:3072].partition_broadcast(128)), "bgb",
              (), (r_bgb,))
        for kc in range(8):
            for s_ in range(2):
                S.op("act", I("activation", srep[:, kc * 2 + s_, :], ones_f[:], AF.Copy,
                                                                 scale=cT[:, kc, s_:s_ + 1]),
                     (r_cT, r_const), (r_srep,))
        gb = [(banks[5], r_bank[5]), (banks[6], r_bank[6]), (banks[0], r_bank[0]), (banks[1], r_bank[1])]
        for kc in range(8):
            st = kc % NST
            stv = wst[:, st].rearrange("p a b -> p (a b)")
            src = wmod_d[l, kc * 128:(kc + 1) * 128, 2048:3072]
            S.dma(I("dma_start", out=stv, in_=src), "wst%d" % st, (), (r_wst[st],))
            for s_ in range(2):
                for hf in range(2):
                    bk, rb = gb[s_ * 2 + hf]
                    mm(bk[:], srep[:, kc * 2 + s_, :], stv[:, hf * 512:(hf + 1) * 512], kc == 0, kc == 7,
                       (r_srep, r_wst[st]), (rb,))
        for s_ in range(2):
            for hf in range(2):
                bk, rb = gb[s_ * 2 + hf]
                S.op("dve", I("tensor_tensor",
                    GT[:, s_, hf * 512:(hf + 1) * 512], bk[:], bgb[:, hf * 512:(hf + 1) * 512], ALU.add),
                    (rb, r_bgb), (r_GT,))
        if not last:
            load_layer_small(l + 1)
            mods_ss(l + 1)
        ntiles = 16 if last else 18

        def mm_part(t):
            i = t % 2
            s_ = 0 if t < 16 else 1
            if l == 0:
                src = x_d[b, t * 128:(t + 1) * 128, :] if t < 16 else ctx_d[b, (t - 16) * 128:(t - 15) * 128, :]
            else:
                src = xs_d[t * 128:(t + 1) * 128, :]
            S.dma(I("dma_start", out=xt[i][:], in_=src), "xt%d" % i, (), (r_xt[i],))
            for hf in range(2):
                bk, rb = next_bank()
                for cb in range(4):
                    ob = hf * 4 + cb
                    wob, rwo = U.ap(uo[ob])
                    for j in range(8):
                        mm(bk[:, cb * 128:(cb + 1) * 128], Mb[:, j, t * 128:(t + 1) * 128], wob[:, j, :],
                           j == 0 and cb == 0, j == 7, (rwo, r_M[j][t // 4 if t < 16 else 4]), (rb,), skip=True)
                S.op("dve", I("tensor_tensor",
                    xo[i][:, hf * 512:(hf + 1) * 512], bk[:], GT[:, s_, hf * 512:(hf + 1) * 512], ALU.mult),
                    (rb, r_GT), (r_xo[i],))
            S.op("dve", I("tensor_tensor", xo[i][:], xo[i][:], xt[i][:], ALU.add),
                 (r_xo[i], r_xt[i]), (r_xo[i],))
            if not last:
                S.dma(I("dma_start", out=xs_d[t * 128:(t + 1) * 128, :], in_=xo[i][:]),
                      "xo%d" % i, (r_xo[i],), ())

        def norm_part(t):
            i = t % 2
            if not last:
                norm_tile(xo[i][:], r_xo[i], t, (junk[:], ssm[i][:, 0:1], xn[:], r_ss[i], r_xn))
            else:
                ss = ssm[i][:, 0:1]
                S.op("act", I("activation", junk[:], xo[i][:], AF.Square, accum_out=ss),
                     (r_xo[i],), (r_ss[i], r_junk))
                S.op("dve", I("tensor_scalar", ss, ss, 1.0 / D, NORM_EPS, ALU.mult, ALU.add),
                     (r_ss[i],), (r_ss[i],))
                S.op("act", I("activation", ss, ss, AF.Sqrt), (r_ss[i],), (r_ss[i],))
                S.op("dve", I("reciprocal", ss, ss), (r_ss[i],), (r_ss[i],))
                S.op("dve", I("scalar_tensor_tensor",
                    xo[i][:], xo[i][:], ss, fnw_b[:], ALU.mult, ALU.mult), (r_xo[i], r_ss[i], r_const), (r_xo[i],))
                S.dma(I("dma_start", out=out_d[b, t * 128:(t + 1) * 128, :], in_=xo[i][:]),
                      "xo%d" % i, (r_xo[i],), ())

        for t in range(ntiles + 1):
            if t < ntiles:
                mm_part(t)
            if t >= 1:
                norm_part(t - 1)
        for u in uo:
            U.release(u)

    def phase_in(b):
        S.barrier()
        cv = Carver()
        xt = [cv.take([D]) for _ in range(2)]
        xn = cv.take([D])
        junk = cv.take([D], BF16)
        ssm = [cv.take([8]) for _ in range(2)]
        r_xt = [Reg(), Reg()]
        r_xn = Reg()
        r_ss = [Reg(), Reg()]
        load_c(b)
        load_layer_small(0)
        mods_ss(0)
        for t in range(18):
            i = t % 2
            src = x_d[b, t * 128:(t + 1) * 128, :] if t < 16 else ctx_d[b, (t - 16) * 128:(t - 15) * 128, :]
            S.dma(I("dma_start", out=xt[i][:], in_=src), "xt%d" % i, (), (r_xt[i],))
            norm_tile(xt[i][:], r_xt[i], t, (junk[:], ssm[i][:, 0:1], xn[:], r_ss[i], r_xn))

    for b in range(NB):
        phase_in(b)
        prefetch_attn(0)
        for l in range(NL):
            last = (l == NL - 1)
            compute_lam(l)
            phase_attn(l, last)
            prefetch_merge(l, 0)
            phase_merge(l, 0, wa_d, last)
            for c in range(4):
                req_win(l, (4 + c) * 1024, pre=True)
            phase_conv(l, last)
            prefetch_merge(l, 1)
            phase_merge(l, 1, wc_d, last)
            req_win(l, 8192, pre=True)
            req_win(l, 9216, pre=True)
            phase_fourier(l, last)
            prefetch_merge(l, 2)
            phase_merge(l, 2, wf_d, last)
            for ob in range(8):
                req_wo(l, ob, pre=True)
            phase_out(b, l, last)
            if not last:
                prefetch_attn(l + 1)

    dma_names = sorted(S.dma_cnt.keys())
    with ExitStack() as es2:
        esem = {e: es2.enter_context(nc.semaphore("s_" + e)) for e in ("pe", "act", "dve", "pool")}
        dsem = {n: es2.enter_context(nc.semaphore("d_" + n)) for n in dma_names}
        block = es2.enter_context(nc.Block())
        S.emit(nc, block, esem, dsem, final_waits=dma_names)
    es.close()
    return nc


def _tables():
    bf = ml_dtypes.bfloat16
    t = {}
    n = np.arange(SEQ, dtype=np.int64)
    nk = (n[:, None] * n[None, :]) % SEQ
    ang = 2.0 * np.pi * nk.astype(np.float64) / SEQ
    t["t_dftc"] = np.cos(ang).astype(np.float32).astype(bf)
    t["t_dfts"] = np.sin(ang).astype(np.float32).astype(bf)
    n = np.arange(CTX, dtype=np.int64)
    nk = (n[:, None] * n[None, :]) % CTX
    ang = 2.0 * np.pi * nk.astype(np.float64) / CTX
    t["t_dftc_c"] = np.cos(ang).astype(np.float32).astype(bf)
    t["t_dfts_c"] = np.sin(ang).astype(np.float32).astype(bf)
    c = np.arange(128, dtype=np.int64)
    cm = (c[:, None] * c[None, :]) % 128
    ang = 2.0 * np.pi * cm.astype(np.float64) / 128
    t["t_csc"] = np.concatenate([np.cos(ang), -np.sin(ang)], axis=1).astype(np.float32).astype(bf)
    tok = np.arange(SEQ)
    row = (tok // 64).astype(np.float32)
    col = (tok % 64).astype(np.float32)
    inv_freq = (10000.0 ** (-np.arange(0, 32, 2, dtype=np.float32) / np.float32(32))).astype(np.float32)
    ang_r = (row[:, None] * inv_freq[None, :]).astype(np.float32)
    ang_c = (col[:, None] * inv_freq[None, :]).astype(np.float32)
    cosr, sinr = np.cos(ang_r).astype(np.float32), np.sin(ang_r).astype(np.float32)
    cosc, sinc = np.cos(ang_c).astype(np.float32), np.sin(ang_c).astype(np.float32)
    C64 = np.concatenate([cosr, cosr, cosc, cosc], axis=1).T
    S64 = np.concatenate([-sinr, sinr, -sinc, sinc], axis=1).T
    t["t_ropec"] = np.ascontiguousarray(np.concatenate([C64, C64], axis=0)).astype(np.float32)
    t["t_ropes"] = np.ascontiguousarray(np.concatenate([S64, S64], axis=0)).astype(np.float32)
    t["t_identb"] = np.eye(128, dtype=np.float32).astype(bf)
    t["t_identf"] = np.eye(128, dtype=np.float32)
    P = np.zeros((128, 128), dtype=np.float32)
    for p in range(128):
        d = p % 64
        partner = p + 16 if (d % 32) < 16 else p - 16
        P[partner, p] = 1.0
    t["t_perm"] = P
    t["t_permb"] = P.astype(bf)
    return t


_CACHE = {}
import os as _os
FLAGS = {k: (_os.environ.get("KF_" + k.upper(), "1") == "1") for k in ("pipe", "defer", "accdb", "allb", "fin")}
FLAGS["rope"] = (_os.environ.get("KF_ROPE", "1") == "1")


def kernel(x, c, ctx, c_ctx, norm_w, w_mod, b_mod, w_in, lambda_qk, subln_w, conv_w,
           w_attn_o, w_conv_o, w_four_o, w_out, final_norm_w, _nb=2, _nl=DEPTH, _ncores=N_CORES):
    f = lambda a: np.ascontiguousarray(np.asarray(a, dtype=np.float32))
    x, c, ctx, c_ctx = f(x), f(c), f(ctx), f(c_ctx)
    shared = {
        "c_ctx": c_ctx, "norm_w": f(norm_w), "w_mod": f(w_mod), "b_mod": f(b_mod), "w_in": f(w_in),
        "lambda_qk": f(lambda_qk).reshape(DEPTH, 256), "subln_w": f(subln_w), "conv_w": f(conv_w),
        "w_attn_o": f(w_attn_o), "w_conv_o": f(w_conv_o), "w_four_o": f(w_four_o), "w_out": f(w_out),
        "final_norm_w": f(final_norm_w),
    }
    shared.update(_tables())
    key = (_nb, _nl)
    if key not in _CACHE:
        _CACHE[key] = build_program(_nb, _nl)
    nc = _CACHE[key]
    in_maps = []
    for i in range(_ncores):
        m = dict(shared)
        m["x"] = x[i * _nb:(i + 1) * _nb]
        m["ctx"] = ctx[i * _nb:(i + 1) * _nb]
        m["c"] = c[i * _nb:(i + 1) * _nb]
        in_maps.append(m)
    res = run_bass_kernel_spmd(nc, in_maps, core_ids=list(range(_ncores)))
    return np.concatenate([r["out"] for r in res.results], axis=0)
```
